# Optimizing a Trainium2 kernel written in Bass

```python
import math
import jax, jax.numpy as jnp
from jax import lax
import numpy as np

D_MODEL = 1024
BATCH = 16
SEQ = 2048
DEPTH = 2

CHUNK = 64
Q_BLOCK = 128
N_MEM = 256
D_LRU = 512
LRU_BLOCKS = 8
LRU_BLOCK_DIM = D_LRU // LRU_BLOCKS
LRU_CONV = 4
LRU_C = 8.0
D_CONV = 512
CONV_WIDTH = 31
DA_HEADS = 4
DA_DIM = 64
DA_VDIM = 2 * DA_DIM
D_DA = DA_HEADS * DA_VDIM
REL_BUCKETS = 32
REL_MAX_DIST = 128
XA_HEADS = 4
XA_DIM = D_MODEL // XA_HEADS
D_FF = 2816
FFN_CONV = 3
N_BRANCH = 3
EPS = 1e-6

LRU_X_END = D_LRU
LRU_G_END = 2 * D_LRU
CONV_END = LRU_G_END + 2 * D_CONV
Q_END = CONV_END + DA_HEADS * 2 * DA_DIM
K_END = Q_END + DA_HEADS * 2 * DA_DIM
V_END = K_END + D_DA
D_IN = V_END

kernel_name = "hybrid_rglru_conformer_diffattn_block"


def rms_norm(x, g):
    xf = x.astype(jnp.float32)
    y = xf * lax.rsqrt(jnp.mean(xf * xf, axis=-1, keepdims=True) + EPS)
    return (y * g.astype(jnp.float32)).astype(x.dtype)


def layer_norm(x, g, b):
    xf = x.astype(jnp.float32)
    mu = jnp.mean(xf, axis=-1, keepdims=True)
    xc = xf - mu
    y = xc * lax.rsqrt(jnp.mean(xc * xc, axis=-1, keepdims=True) + EPS)
    return (y * g.astype(jnp.float32) + b.astype(jnp.float32)).astype(x.dtype)


def causal_dwconv(x, w, b):
    K, C = w.shape
    y = lax.conv_general_dilated(
        x, w[:, None, :].astype(x.dtype), window_strides=(1,), padding=[(K - 1, 0)],
        dimension_numbers=('NWC', 'WIO', 'NWC'), feature_group_count=C)
    return (y + b.astype(x.dtype)).astype(x.dtype)


def t5_bucket(rel):
    nb = REL_BUCKETS // 2
    ret = jnp.where(rel > 0, nb, 0)
    n = jnp.abs(rel)
    max_exact = nb // 2
    large = max_exact + (jnp.log(jnp.maximum(n, 1).astype(jnp.float32) / max_exact)
                         / math.log(REL_MAX_DIST / max_exact) * (nb - max_exact)).astype(jnp.int32)
    large = jnp.minimum(large, nb - 1)
    return ret + jnp.where(n < max_exact, n, large)


def rg_lru(x, wr, br, wi, bi, lam):
    B_, S_, _ = x.shape
    xf = x.astype(jnp.float32)
    xb = xf.reshape(B_, S_, LRU_BLOCKS, LRU_BLOCK_DIM)
    r = jax.nn.sigmoid(jnp.einsum('bshi,hij->bshj', xb, wr.astype(jnp.float32)).reshape(B_, S_, D_LRU) + br)
    i = jax.nn.sigmoid(jnp.einsum('bshi,hij->bshj', xb, wi.astype(jnp.float32)).reshape(B_, S_, D_LRU) + bi)
    log_a = -LRU_C * r * jax.nn.softplus(-lam.astype(jnp.float32))
    a = jnp.exp(log_a)
    u = jnp.sqrt(-jnp.expm1(2.0 * log_a)) * (i * xf)

    def combine(left, right):
        a1, b1 = left
        a2, b2 = right
        return a1 * a2, a2 * b1 + b2

    _, h = lax.associative_scan(combine, (a, u), axis=1)
    return h.astype(x.dtype)


def diff_attention(q, k, v, lam, lam_init, subln_g, rel_bias):
    B_, S_ = q.shape[0], q.shape[1]
    scale = DA_DIM ** -0.5
    qf = q.astype(jnp.float32) * scale
    kf = k.astype(jnp.float32)
    vf = v.astype(jnp.float32)
    neg = jnp.finfo(jnp.float32).min
    outs = []
    for blk in range(S_ // Q_BLOCK):
        q0, q1 = blk * Q_BLOCK, (blk + 1) * Q_BLOCK
        kb, vb = kf[:, :q1], vf[:, :q1]
        s = jnp.einsum('bqhcd,bkhcd->bchqk', qf[:, q0:q1], kb)
        qpos = jnp.arange(q0, q1, dtype=jnp.int32)
        kpos = jnp.arange(q1, dtype=jnp.int32)
        bias = jnp.transpose(rel_bias.astype(jnp.float32)[t5_bucket(kpos[None, :] - qpos[:, None])], (2, 0, 1))
        allowed = (kpos[None, :] // CHUNK) <= (qpos[:, None] // CHUNK)
        p = jax.nn.softmax(jnp.where(allowed, s + bias, neg), axis=-1)
        attn = p[:, 0] - lam * p[:, 1]
        outs.append(jnp.einsum('bhqk,bkhe->bqhe', attn, vb))
    o = jnp.concatenate(outs, axis=1)
    o = rms_norm(o, subln_g) * (1.0 - lam_init)
    return o.reshape(B_, S_, D_DA).astype(q.dtype)


def setup_inputs(seed: int = 0) -> dict:
    key = jax.random.key(seed)
    ks = iter(jax.random.split(key, 48))
    L = DEPTH

    def nrm(shape, scale):
        return jax.random.normal(next(ks), shape, jnp.float32) * scale

    def gain(shape):
        return 1.0 + nrm(shape, 0.02)

    a0 = jax.random.uniform(next(ks), (L, D_LRU), jnp.float32, 0.9, 0.999)
    lru_lambda = jnp.log(a0 / (1.0 - a0))
    return {
        "x": nrm((BATCH, SEQ, D_MODEL), 1.0),
        "mem": nrm((BATCH, N_MEM, D_MODEL), 1.0),
        "rel_bias": nrm((REL_BUCKETS, DA_HEADS), 0.5),
        "norm_mix_g": gain((L, D_MODEL)),
        "w_in": nrm((L, D_MODEL, D_IN), D_MODEL ** -0.5),
        "w_gate": nrm((L, N_BRANCH, D_MODEL, D_MODEL), D_MODEL ** -0.5),
        "b_gate": nrm((L, N_BRANCH, D_MODEL), 0.02),
        "lru_conv_w": nrm((L, LRU_CONV, D_LRU), LRU_CONV ** -0.5),
        "lru_conv_b": nrm((L, D_LRU), 0.02),
        "lru_wr": nrm((L, LRU_BLOCKS, LRU_BLOCK_DIM, LRU_BLOCK_DIM), LRU_BLOCK_DIM ** -0.5),
        "lru_br": nrm((L, D_LRU), 0.02),
        "lru_wi": nrm((L, LRU_BLOCKS, LRU_BLOCK_DIM, LRU_BLOCK_DIM), LRU_BLOCK_DIM ** -0.5),
        "lru_bi": nrm((L, D_LRU), 0.02),
        "lru_lambda": lru_lambda,
        "lru_out": nrm((L, D_LRU, D_MODEL), D_LRU ** -0.5),
        "cm_conv_w": nrm((L, CONV_WIDTH, D_CONV), CONV_WIDTH ** -0.5),
        "cm_conv_b": nrm((L, D_CONV), 0.02),
        "cm_ln_g": gain((L, D_CONV)),
        "cm_ln_b": nrm((L, D_CONV), 0.02),
        "cm_out": nrm((L, D_CONV, D_MODEL), D_CONV ** -0.5),
        "da_lambda": nrm((L, 4, DA_DIM), 0.1),
        "da_subln_g": gain((L, DA_VDIM)),
        "da_out": nrm((L, D_DA, D_MODEL), D_DA ** -0.5),
        "w_o": nrm((L, D_MODEL, D_MODEL), D_MODEL ** -0.5),
        "norm_xa_g": gain((L, D_MODEL)),
        "norm_mem_g": gain((L, D_MODEL)),
        "xa_wq": nrm((L, D_MODEL, D_MODEL), D_MODEL ** -0.5),
        "xa_wkv": nrm((L, D_MODEL, 2 * D_MODEL), D_MODEL ** -0.5),
        "xa_wo": nrm((L, D_MODEL, D_MODEL), D_MODEL ** -0.5),
        "norm_ffn_g": gain((L, D_MODEL)),
        "ffn_w1": nrm((L, D_MODEL, D_FF), D_MODEL ** -0.5),
        "ffn_w3": nrm((L, D_MODEL, D_FF), D_MODEL ** -0.5),
        "ffn_conv_w": nrm((L, FFN_CONV, D_FF), FFN_CONV ** -0.5),
        "ffn_conv_b": nrm((L, D_FF), 0.02),
        "ffn_w2": nrm((L, D_FF, D_MODEL), D_FF ** -0.5),
        "final_g": gain((D_MODEL,)),
    }


def reference(x, mem, rel_bias, norm_mix_g, w_in, w_gate, b_gate, lru_conv_w, lru_conv_b,
              lru_wr, lru_br, lru_wi, lru_bi, lru_lambda, lru_out, cm_conv_w, cm_conv_b,
              cm_ln_g, cm_ln_b, cm_out, da_lambda, da_subln_g, da_out, w_o, norm_xa_g,
              norm_mem_g, xa_wq, xa_wkv, xa_wo, norm_ffn_g, ffn_w1, ffn_w3, ffn_conv_w,
              ffn_conv_b, ffn_w2, final_g):
    B_, S_, _ = x.shape
    M_ = mem.shape[1]
    for l in range(DEPTH):
        h = rms_norm(x, norm_mix_g[l])
        z = h @ w_in[l]
        xa = causal_dwconv(z[..., :LRU_X_END], lru_conv_w[l], lru_conv_b[l])
        ya = rg_lru(xa, lru_wr[l], lru_br[l], lru_wi[l], lru_bi[l], lru_lambda[l])
        ya = ya * jax.nn.gelu(z[..., LRU_X_END:LRU_G_END])
        cz = z[..., LRU_G_END:CONV_END]
        c = cz[..., :D_CONV] * jax.nn.sigmoid(cz[..., D_CONV:])
        c = causal_dwconv(c, cm_conv_w[l], cm_conv_b[l])
        yb = jax.nn.silu(layer_norm(c, cm_ln_g[l], cm_ln_b[l]))
        q = z[..., CONV_END:Q_END].reshape(B_, S_, DA_HEADS, 2, DA_DIM)
        k = z[..., Q_END:K_END].reshape(B_, S_, DA_HEADS, 2, DA_DIM)
        v = z[..., K_END:V_END].reshape(B_, S_, DA_HEADS, DA_VDIM)
        lam_init = 0.8 - 0.6 * math.exp(-0.3 * l)
        lam_vec = da_lambda[l].astype(jnp.float32)
        lam = (jnp.exp(jnp.sum(lam_vec[0] * lam_vec[1])) - jnp.exp(jnp.sum(lam_vec[2] * lam_vec[3]))
               + lam_init)
        yc = diff_attention(q, k, v, lam, lam_init, da_subln_g[l], rel_bias)
        merged = (jax.nn.sigmoid(h @ w_gate[l, 0] + b_gate[l, 0]) * (ya @ lru_out[l])
                  + jax.nn.sigmoid(h @ w_gate[l, 1] + b_gate[l, 1]) * (yb @ cm_out[l])
                  + jax.nn.sigmoid(h @ w_gate[l, 2] + b_gate[l, 2]) * (yc @ da_out[l]))
        x = x + (merged @ w_o[l]).astype(x.dtype)

        hq = rms_norm(x, norm_xa_g[l])
        m = rms_norm(mem, norm_mem_g[l])
        qx = (hq @ xa_wq[l]).reshape(B_, S_, XA_HEADS, XA_DIM).astype(jnp.float32)
        kv = (m @ xa_wkv[l]).reshape(B_, M_, 2, XA_HEADS, XA_DIM).astype(jnp.float32)
        s = jnp.einsum('bqhd,bkhd->bhqk', qx, kv[:, :, 0]) * (XA_DIM ** -0.5)
        p = jax.nn.softmax(s, axis=-1)
        o = jnp.einsum('bhqk,bkhd->bqhd', p, kv[:, :, 1]).reshape(B_, S_, D_MODEL).astype(x.dtype)
        x = x + (o @ xa_wo[l]).astype(x.dtype)

        hf = rms_norm(x, norm_ffn_g[l])
        a = causal_dwconv(hf @ ffn_w1[l], ffn_conv_w[l], ffn_conv_b[l])
        x = x + ((jax.nn.silu(a) * (hf @ ffn_w3[l])) @ ffn_w2[l]).astype(x.dtype)
    return rms_norm(x, final_g)
```

```python
import math
from collections import defaultdict
from contextlib import ExitStack

import numpy as np
import concourse.bass as bass
import concourse.mybir as mybir
from concourse.bass_utils import run_bass_kernel_spmd

F32 = mybir.dt.float32
BF16 = mybir.dt.bfloat16
AF = mybir.ActivationFunctionType
ALU = mybir.AluOpType
AX = mybir.AxisListType
ESZ = {F32: 4, BF16: 2}
PAGE = 2048
SB_BASE = 16512
SB_END = 229344


class Sched:
    ENGS = ('pe', 'act', 'dve', 'pool', 'sp')

    def __init__(self, nc):
        self.nc = nc
        self.prog = {e: [] for e in self.ENGS}
        self.count = defaultdict(int)
        self.known = {e: defaultdict(int) for e in self.ENGS}
        self.snap = {}
        self.recs = defaultdict(lambda: defaultdict(list))
        self.tinfo = {}
        self.dma_keys = {}
        self.nwaits = 0
        self.nops = 0
        self.uid = 0

    def sbuf(self, name, shape, dtype, at):
        esz = ESZ[dtype]
        nbytes = int(np.prod(shape[1:])) * esz
        assert at >= SB_BASE and at + nbytes <= SB_END, (name, at, nbytes)
        self.uid += 1
        h = self.nc.alloc_sbuf_tensor_at(f"{name}_{self.uid}", list(shape), dtype, offset=at)
        self.tinfo[h.name] = ('sbuf', at, esz)
        return h

    def psum_banks(self):
        banks = []
        for i in range(8):
            h = self.nc.alloc_psum_tensor(f"psb{i}", [128, 512], F32)
            self.tinfo[h.name] = ('psum', i * 2048, 4)
            banks.append(h)
        return banks

    def dram(self, name, shape, dtype, kind="Internal"):
        h = self.nc.dram_tensor(name, list(shape), dtype, kind=kind)
        self.tinfo[h.name] = (None if kind == "ExternalInput" else 'dram:' + name, 0, ESZ[dtype])
        return h

    def _box(self, ap):
        space, base, esz = self.tinfo[ap.tensor.name]
        if space is None:
            return None
        pat = ap.ap
        off = int(ap.offset)
        if space.startswith('dram'):
            ext = sum((c - 1) * s for s, c in pat)
            return space, (0, 1, off * esz, (off + ext + 1) * esz)
        pcnt = pat[0][1]
        fsz = int(np.prod(ap.tensor.shape[1:]))
        p0 = off // fsz
        f0 = off % fsz
        if space == 'psum':
            return space, (0, 128, base, base + 2048)
        ext = sum((c - 1) * s for s, c in pat[1:])
        return space, (p0, p0 + pcnt, base + f0 * esz, base + (f0 + ext + 1) * esz)

    @staticmethod
    def _pages(space, box):
        if space.startswith('dram'):
            return (0,)
        return range(box[2] // PAGE, (box[3] - 1) // PAGE + 1)

    @staticmethod
    def _ovl(a, b):
        return a[0] < b[1] and b[0] < a[1] and a[2] < b[3] and b[2] < a[3]

    def _deps(self, reads, writes):
        deps = set()
        racc = [b for b in (self._box(ap) for ap in reads) if b is not None]
        wacc = [b for b in (self._box(ap) for ap in writes) if b is not None]
        wacc += [b for b in racc if b[0] == 'psum']
        racc = [b for b in racc if b[0] != 'psum']
        for space, box in racc:
            seen = set()
            sp = self.recs[space]
            for pg in self._pages(space, box):
                for r in sp[pg]:
                    if id(r) in seen:
                        continue
                    seen.add(id(r))
                    if r[1] and self._ovl(r[0], box):
                        deps.add((r[2], True))
        for space, box in wacc:
            seen = set()
            sp = self.recs[space]
            for pg in self._pages(space, box):
                for r in sp[pg]:
                    if id(r) in seen:
                        continue
                    seen.add(id(r))
                    if self._ovl(r[0], box):
                        deps.add((r[2], False))
        return deps, racc, wacc

    def _record(self, ev, racc, wacc):
        for space, box in wacc:
            pgs = self._pages(space, box)
            sp = self.recs[space]
            for pg in pgs:
                lst = sp[pg]
                lst[:] = [r for r in lst if not (box[0] <= r[0][0] and r[0][1] <= box[1]
                                                 and box[2] <= r[0][2] and r[0][3] <= box[3])]
            rec = (box, True, ev)
            for pg in pgs:
                sp[pg].append(rec)
        for space, box in racc:
            pgs = self._pages(space, box)
            sp = self.recs[space]
            rec = (box, False, ev)
            for pg in pgs:
                lst = sp[pg]
                lst[:] = [r for r in lst if not ((not r[1]) and r[0] == box and r[2][0] == ev[0])]
                lst.append(rec)

    def _emit_waits(self, eng, deps):
        kn = self.known[eng]
        need = {}
        for (key, n), raw in deps:
            if key == eng and eng == 'pe' and not raw:
                continue
            if kn[key] >= n:
                continue
            if need.get(key, 0) < n:
                need[key] = n
        for key, n in need.items():
            if kn[key] >= n:
                continue
            self.prog[eng].append(('wait', key, n))
            self.nwaits += 1
            sn = self.snap.get((key, n))
            if sn is not None:
                for k2, v2 in sn.items():
                    if kn[k2] < v2:
                        kn[k2] = v2
            if kn[key] < n:
                kn[key] = n

    def op(self, eng, fn, reads=(), writes=()):
        deps, racc, wacc = self._deps(reads, writes)
        self._emit_waits(eng, deps)
        self.count[eng] += 1
        ev = (eng, self.count[eng])
        self.snap[ev] = dict(self.known[eng])
        self.prog[eng].append(('op', fn))
        self._record(ev, racc, wacc)
        self.nops += 1
        return ev

    def dma(self, queue, key, pairs, **kw):
        dkey = 'dma:' + key
        if dkey not in self.dma_keys:
            self.dma_keys[dkey] = queue
        assert self.dma_keys[dkey] == queue
        prev = self.count[dkey]
        if prev and self.known[queue][dkey] < prev:
            self._emit_waits(queue, {((dkey, prev), True)})
        alld = set()
        accs = []
        for out, in_ in pairs:
            deps, racc, wacc = self._deps([in_], [out])
            alld |= deps
            accs.append((racc, wacc))
        self._emit_waits(queue, alld)
        self.count[dkey] += len(pairs)
        ev = (dkey, self.count[dkey])
        self.snap[ev] = dict(self.known[queue])
        for (out, in_), (racc, wacc) in zip(pairs, accs):
            self.prog[queue].append(('dma', out, in_, dkey, kw))
            self._record(ev, racc, wacc)
        return ev

    def finish(self, eng='sp'):
        deps = set()
        for dkey in self.dma_keys:
            if self.count[dkey]:
                deps.add(((dkey, self.count[dkey]), True))
        for e in self.ENGS:
            if e != eng and self.count[e]:
                deps.add(((e, self.count[e]), True))
        self._emit_waits(eng, deps)

    def emit(self):
        nc = self.nc
        keys = list(self.ENGS) + list(self.dma_keys.keys())
        with ExitStack() as st:
            sems = {}
            for k in keys:
                sems[k] = st.enter_context(nc.semaphore(k.replace(':', '_')))
            block = st.enter_context(nc.Block())
            engmap = {'pe': block.tensor, 'act': block.scalar, 'dve': block.vector,
                      'pool': block.gpsimd, 'sp': block.sync}
            for e in self.ENGS:
                prog = self.prog[e]
                esem = sems[e]

                def body(eobj, prog=prog, esem=esem):
                    for item in prog:
                        if item[0] == 'wait':
                            _, key, n = item
                            eobj.wait_ge(sems[key], n * (16 if key.startswith('dma:') else 1))
                        elif item[0] == 'op':
                            item[1](eobj).then_inc(esem, 1)
                        else:
                            _, out, in_, dkey, kw = item
                            eobj.dma_start(out=out, in_=in_, **kw).then_inc(sems[dkey], 16)
                engmap[e](body)


D = 1024
KC = 8
T = 512
NMEM = 256
DFF = 2816
FC = DFF // 128
NLAYER = 2
EPS = 1e-6
WT = 4096
NSLOT = 3

NPC = 320
O_GMIX, O_BGATE, O_LCW, O_LCB, O_LBR, O_LBI, O_LLAM = 0, 8, 32, 48, 52, 56, 60
O_CCW, O_CCB, O_CLG, O_CLB, O_DSG, O_GXA, O_GMEM, O_GFFN, O_FCW, O_FCB = 64, 188, 192, 196, 200, 201, 209, 217, 225, 291
DR_S8, DR_S16, DR_LAM, DR_NLAM, DR_DSG, DR_HBR, DR_HBI = 0, 4, 8, 9, 10, 12, 16

FGROUPS = [(g * 4, min(4, FC - g * 4)) for g in range((FC + 3) // 4)]


def tile_specs():
    sp = [('in_ca', 8, 512), ('in_cg', 8, 512), ('in_lx', 8, 512), ('in_lg', 8, 512),
          ('gate_a0', 8, 512), ('out_a0', 4, 512), ('gate_a1', 8, 512), ('out_a1', 4, 512),
          ('gate_b0', 8, 512), ('out_b0', 4, 512), ('gate_b1', 8, 512), ('out_b1', 4, 512)]
    sp += [(f'in_qkv{h}', 8, 384) for h in range(4)]
    sp += [('gate_c0', 8, 512), ('out_c0', 4, 512), ('gate_c1', 8, 512), ('out_c1', 4, 512),
           ('wo_0', 8, 512), ('wo_1', 8, 512),
           ('xk_0', 8, 512), ('xk_1', 8, 512), ('xv_0', 8, 512), ('xv_1', 8, 512)]
    sp += [(f'xq{h}', 8, 256) for h in range(4)]
    sp += [('xo_0', 8, 512), ('xo_1', 8, 512)]
    for g, (c0, n) in enumerate(FGROUPS):
        sp += [(f'w1_{g}', 8, n * 128), (f'w3_{g}', 8, n * 128), (f'w2_{g}', n, 1024)]
    return sp


TILE_SPECS = tile_specs()
TILE_OFF = {}
_o = 0
for _n, _k, _c in TILE_SPECS:
    TILE_OFF[_n] = (_o, _k, _c)
    _o += _k * _c
LBD_OFF = _o
_o += 1024
NPT = _o


def _kmajor(w):
    K, N = w.shape
    return np.ascontiguousarray(w.reshape(K // 128, 128, N).transpose(1, 0, 2))


def _cols(v):
    return np.ascontiguousarray(v.reshape(-1, 128).T)


def pack_layer_weights(inp, l):
    w_in = inp['w_in'][l]
    mats = {
        'in_lx': w_in[:, 0:512], 'in_lg': w_in[:, 512:1024],
        'out_a0': inp['lru_out'][l][:, 0:512], 'out_a1': inp['lru_out'][l][:, 512:1024],
        'out_b0': inp['cm_out'][l][:, 0:512], 'out_b1': inp['cm_out'][l][:, 512:1024],
        'out_c0': inp['da_out'][l][:, 0:512], 'out_c1': inp['da_out'][l][:, 512:1024],
        'gate_a0': inp['w_gate'][l, 0][:, 0:512], 'gate_a1': inp['w_gate'][l, 0][:, 512:1024],
        'in_ca': w_in[:, 1024:1536], 'in_cg': w_in[:, 1536:2048],
        'gate_b0': inp['w_gate'][l, 1][:, 0:512], 'gate_b1': inp['w_gate'][l, 1][:, 512:1024],
        'gate_c0': inp['w_gate'][l, 2][:, 0:512], 'gate_c1': inp['w_gate'][l, 2][:, 512:1024],
        'wo_0': inp['w_o'][l][:, 0:512], 'wo_1': inp['w_o'][l][:, 512:1024],
        'xk_0': inp['xa_wkv'][l][:, 0:512], 'xk_1': inp['xa_wkv'][l][:, 512:1024],
        'xv_0': inp['xa_wkv'][l][:, 1024:1536], 'xv_1': inp['xa_wkv'][l][:, 1536:2048],
        'xo_0': inp['xa_wo'][l][:, 0:512], 'xo_1': inp['xa_wo'][l][:, 512:1024],
    }
    for h in range(4):
        mats[f'in_qkv{h}'] = np.concatenate([w_in[:, 2048 + h * 128: 2048 + (h + 1) * 128],
                                             w_in[:, 2560 + h * 128: 2560 + (h + 1) * 128],
                                             w_in[:, 3072 + h * 128: 3072 + (h + 1) * 128]], axis=1)
        mats[f'xq{h}'] = inp['xa_wq'][l][:, h * 256:(h + 1) * 256]
    for g, (c0, n) in enumerate(FGROUPS):
        mats[f'w1_{g}'] = inp['ffn_w1'][l][:, c0 * 128:(c0 + n) * 128]
        mats[f'w3_{g}'] = inp['ffn_w3'][l][:, c0 * 128:(c0 + n) * 128]
        mats[f'w2_{g}'] = inp['ffn_w2'][l][c0 * 128:(c0 + n) * 128, :]
    out = np.zeros((128, NPT), np.float32)
    for name, kc, cols in TILE_SPECS:
        off = TILE_OFF[name][0]
        m = mats[name]
        assert m.shape == (kc * 128, cols), (name, m.shape)
        out[:, off:off + kc * cols] = _kmajor(m).reshape(128, kc * cols)
    bd = np.zeros((128, 8, 128), np.float32)
    for gi, key in enumerate(('lru_wr', 'lru_wi')):
        w = inp[key][l]
        for c in range(4):
            for hb in range(2):
                bd[hb * 64:(hb + 1) * 64, gi * 4 + c, hb * 64:(hb + 1) * 64] = w[2 * c + hb]
    out[:, LBD_OFF:LBD_OFF + 1024] = bd.reshape(128, 1024)
    return out


def pack_layer_params(inp, l):
    p = np.zeros((128, NPC), np.float32)
    p[:, O_GMIX:O_GMIX + 8] = _cols(inp['norm_mix_g'][l])
    for j in range(3):
        p[:, O_BGATE + j * 8:O_BGATE + (j + 1) * 8] = _cols(inp['b_gate'][l, j])
    for tap in range(4):
        p[:, O_LCW + tap * 4:O_LCW + (tap + 1) * 4] = _cols(inp['lru_conv_w'][l, tap])
    p[:, O_LCB:O_LCB + 4] = _cols(inp['lru_conv_b'][l])
    p[:, O_LBR:O_LBR + 4] = _cols(inp['lru_br'][l])
    p[:, O_LBI:O_LBI + 4] = _cols(inp['lru_bi'][l])
    p[:, O_LLAM:O_LLAM + 4] = _cols(inp['lru_lambda'][l])
    for tap in range(31):
        p[:, O_CCW + tap * 4:O_CCW + (tap + 1) * 4] = _cols(inp['cm_conv_w'][l, tap])
    p[:, O_CCB:O_CCB + 4] = _cols(inp['cm_conv_b'][l])
    p[:, O_CLG:O_CLG + 4] = _cols(inp['cm_ln_g'][l])
    p[:, O_CLB:O_CLB + 4] = _cols(inp['cm_ln_b'][l])
    p[:, O_DSG:O_DSG + 1] = _cols(inp['da_subln_g'][l])
    p[:, O_GXA:O_GXA + 8] = _cols(inp['norm_xa_g'][l])
    p[:, O_GMEM:O_GMEM + 8] = _cols(inp['norm_mem_g'][l])
    p[:, O_GFFN:O_GFFN + 8] = _cols(inp['norm_ffn_g'][l])
    for tap in range(3):
        p[:, O_FCW + tap * FC:O_FCW + (tap + 1) * FC] = _cols(inp['ffn_conv_w'][l, tap])
    p[:, O_FCB:O_FCB + FC] = _cols(inp['ffn_conv_b'][l])
    return p


def t5_bucket_np(rel):
    nb = 16
    ret = np.where(rel > 0, nb, 0)
    n = np.abs(rel)
    max_exact = 8
    lg = np.log(np.maximum(n, 1).astype(np.float32) / np.float32(max_exact)) / np.float32(math.log(128 / max_exact))
    large = max_exact + (lg.astype(np.float32) * np.float32(nb - max_exact)).astype(np.int32)
    large = np.minimum(large, nb - 1)
    return ret + np.where(n < max_exact, n, large)


def onehot_table():
    j = np.arange(768)
    b = t5_bucket_np(127 - j)
    oh = np.zeros((32, 768), np.float32)
    oh[b, j] = 1.0
    return oh


class Builder:
    def __init__(self, S_LEN=2048, NSEQ=2, NL=2, dbg=()):
        self.S_LEN, self.NSEQ, self.NL, self.dbg = S_LEN, NSEQ, NL, tuple(dbg)
        self.NT = S_LEN // T
        self.nc = bass.Bass("TRN2", target_bir_lowering=False)
        self.S = Sched(self.nc)
        self.bank_i = 0
        self.uid = 0

    def pb(self):
        b = self.banks[self.bank_i % 8]
        self.bank_i += 1
        return b

    @staticmethod
    def _aps(*xs):
        return [x for x in xs if isinstance(x, bass.AP)]

    def mm(self, out, lhsT, rhs, start=True, stop=True):
        self.S.op('pe', lambda e: e.matmul(out, lhsT=lhsT, rhs=rhs, start=start, stop=stop),
                  reads=[lhsT, rhs], writes=[out])

    def act(self, out, in_, func, bias=0.0, scale=1.0):
        self.S.op('act', lambda e: e.activation(out=out, in_=in_, func=func, bias=bias, scale=scale),
                  reads=self._aps(in_, bias, scale), writes=[out])

    def tt(self, out, in0, in1, op, eng='dve'):
        self.S.op(eng, lambda e: e.tensor_tensor(out=out, in0=in0, in1=in1, op=op),
                  reads=[in0, in1], writes=[out])

    def ts(self, out, in0, s1, s2, op0, op1=None, eng='dve'):
        if op1 is None:
            self.S.op(eng, lambda e: e.tensor_scalar(out=out, in0=in0, scalar1=s1, scalar2=None, op0=op0),
                      reads=self._aps(in0, s1), writes=[out])
        else:
            self.S.op(eng, lambda e: e.tensor_scalar(out=out, in0=in0, scalar1=s1, scalar2=s2, op0=op0, op1=op1),
                      reads=self._aps(in0, s1, s2), writes=[out])

    def stt(self, out, in0, scalar, in1, op0, op1):
        self.S.op('dve', lambda e: e.scalar_tensor_tensor(out=out, in0=in0, scalar=scalar, in1=in1, op0=op0, op1=op1),
                  reads=self._aps(in0, scalar, in1), writes=[out])

    def copy(self, out, in_, eng='dve'):
        self.S.op(eng, lambda e: e.tensor_copy(out=out, in_=in_), reads=[in_], writes=[out])

    def recip(self, out, in_):
        self.S.op('dve', lambda e: e.reciprocal(out=out, in_=in_), reads=[in_], writes=[out])

    def memset(self, ap, val, eng='dve'):
        self.S.op(eng, lambda e: e.memset(ap, val), reads=[], writes=[ap])

    def scan(self, out, d0, d1, initial):
        self.S.op('dve', lambda e: e.tensor_tensor_scan(out=out, data0=d0, data1=d1, initial=initial,
                                                        op0=ALU.mult, op1=ALU.add),
                  reads=self._aps(d0, d1, initial), writes=[out])

    def alloc(self, name, shape, dtype):
        nbytes = int(np.prod(shape[1:])) * ESZ[dtype]
        at = (self.sc_off + 63) // 64 * 64
        self.sc_off = at + nbytes
        assert self.sc_off <= SB_END, (name, self.sc_off)
        return self.S.sbuf(name, shape, dtype, at)

    def palloc(self, name, shape, dtype):
        nbytes = int(np.prod(shape[1:])) * ESZ[dtype]
        at = (self.p_off + 63) // 64 * 64
        self.p_off = at + nbytes
        return self.S.sbuf(name, shape, dtype, at)

    def w_init(self):
        self.w_sched = []
        for s in range(self.NSEQ):
            for l in range(self.NL):
                for name, kc, cols in TILE_SPECS:
                    self.w_sched.append((l, name, kc, cols))
        self.w_issued = 0
        self.w_next = 0

    def w_issue(self, upto):
        while self.w_issued <= min(upto, len(self.w_sched) - 1):
            i = self.w_issued
            l, name, kc, cols = self.w_sched[i]
            off = TILE_OFF[name][0]
            n = kc * cols
            slot = self.wslots[i % NSLOT]
            pairs = []
            for c0 in range(0, n, 2048):
                c1 = min(n, c0 + 2048)
                pairs.append((slot[:, c0:c1], self.wts[l, :, off + c0: off + c1]))
            self.S.dma('pool', f'w{i % NSLOT}', pairs)
            self.w_issued += 1

    def w_get(self, l, name, hold_prev=False):
        i = self.w_next
        ll, nm, kc, cols = self.w_sched[i]
        assert (ll, nm) == (l, name), (ll, nm, l, name)
        self.w_issue(i + (1 if hold_prev else 2))
        self.w_next += 1
        slot = self.wslots[i % NSLOT]

        def view(k, c0, c1):
            return slot[:, k * cols + c0: k * cols + c1]
        return view

    def build(self):
        S, nc = self.S, self.nc
        NSEQ, NL, S_LEN, NT = self.NSEQ, self.NL, self.S_LEN, self.NT
        self.xT = S.dram("xT", [NSEQ, D, S_LEN], F32, kind="ExternalInput").ap()
        self.memT = S.dram("memT", [NSEQ, D, NMEM], F32, kind="ExternalInput").ap()
        self.wts = S.dram("wts", [NL, 128, NPT], F32, kind="ExternalInput").ap()
        self.prm = S.dram("prm", [NL, 128, NPC], F32, kind="ExternalInput").ap()
        self.gfin = S.dram("gfin", [128, 8], F32, kind="ExternalInput").ap()
        self.relb = S.dram("relb", [32, 4], F32, kind="ExternalInput").ap()
        self.onehot = S.dram("onehot", [32, 768], F32, kind="ExternalInput").ap()
        self.dalam = S.dram("dalam", [1, NL * 256], F32, kind="ExternalInput").ap()
        self.ident = S.dram("ident", [128, 128], F32, kind="ExternalInput").ap()
        self.outT = S.dram("outT", [NSEQ, D, S_LEN], F32, kind="ExternalOutput").ap()
        self.d2h = S.dram("d2", [4, 128, 768], F32, kind="Internal")
        self.dbg_out = {}
        for name in self.dbg:
            if name.startswith('y'):
                self.dbg_out[name] = S.dram("dbg_" + name, [512, S_LEN], BF16, kind="ExternalOutput").ap()
            else:
                self.dbg_out[name] = S.dram("dbg_" + name, [D, S_LEN], F32, kind="ExternalOutput").ap()
        self.p_off = SB_BASE
        self.X = self.palloc("X", [128, KC, S_LEN], F32)
        self.H = self.palloc("H", [128, KC, S_LEN], BF16)
        self.PRM = self.palloc("PRM", [128, NL, NPC], F32)
        self.DER = self.palloc("DER", [128, NL, 32], F32)
        self.GFIN = self.palloc("GFIN", [128, 8], F32)
        self.ONESB = self.palloc("ONESB", [128, 128], BF16)
        self.CST = self.palloc("CST", [128, 4], F32)
        self.STRIP = self.palloc("STRIP", [128, 4, 256], F32)
        self.IDENT = self.palloc("IDENT", [128, 128], BF16)
        self.wslots = [self.palloc(f"WS{i}", [128, WT], BF16) for i in range(NSLOT)]
        self.LBD = [self.palloc(f"LBD{i}", [128, 1024], F32) for i in range(1)]
        self.sc_base = (self.p_off + 63) // 64 * 64
        self.sc_off = self.sc_base
        self.banks = S.psum_banks()
        self.w_init()

        def load_x(s):
            for t in range(NT):
                tt = slice(t * T, (t + 1) * T)
                S.dma('sp', f'xin{t}', [(self.X[:, c, tt], self.xT[s, c * 128:(c + 1) * 128, tt]) for c in range(KC)])

        load_x(0)
        self.setup()
        for s in range(NSEQ):
            if s > 0:
                load_x(s)
            for l in range(NL):
                self.layer(s, l)
            self.final(s)
        S.finish('sp')
        S.emit()
        return nc

    def setup(self):
        S = self.S
        NL = self.NL
        self.sc_off = self.sc_base
        S.dma('sp', 'cst', [(self.PRM[:, l, :], self.prm[l]) for l in range(NL)] + [(self.GFIN[:], self.gfin)])
        S.dma('pool', 'identld', [(self.IDENT[:], self.ident)])
        self.memset(self.ONESB[:], 1.0)
        self.memset(self.CST[:, 0:1], EPS)
        self.memset(self.CST[:, 1:2], 1.0)
        TB = self.alloc("TB", [32, 4], F32)
        OH = self.alloc("OH", [32, 768], F32)
        ONES32 = self.alloc("ONES32", [32, 128], F32)
        LH = self.alloc("LH", [32, 128], F32)
        GB = self.alloc("GB", [128, 768], F32)
        DL = self.alloc("DL", [1, NL * 256], F32)
        TMP = self.alloc("TMPL", [1, 128], F32)
        SS = self.alloc("SSL", [1, 4], F32)
        LV = self.alloc("LV", [1, 2], F32)
        TP = self.alloc("TPS", [128, 4], F32)
        S.dma('sp', 'cst', [(TB[:], self.relb), (OH[:], self.onehot), (DL[:], self.dalam)])
        self.memset(ONES32[:], 1.0)
        d2 = self.d2h.ap()
        for h in range(4):
            self.ts(LH[:], ONES32[:], TB[:, h:h + 1], None, ALU.mult)
            pa, pb2 = self.pb(), self.pb()
            self.mm(pa[:, 0:512], LH[:], OH[:, 0:512])
            self.mm(pb2[:, 0:256], LH[:], OH[:, 512:768])
            self.copy(GB[:, 0:512], pa[:, 0:512])
            self.copy(GB[:, 512:768], pb2[:, 0:256])
            S.dma('sp', 'strip', [(d2[h], GB[:])])
            skew = bass.AP(self.d2h, h * 128 * 768 + 127, [[767, 128], [1, 256]])
            S.dma('sp', 'strip', [(self.STRIP[:, h, :], skew)])
            self.memset(self.STRIP[64:128, h, 0:64], -30000.0)
        for l in range(NL):
            lam_init = 0.8 - 0.6 * math.exp(-0.3 * l)
            self.act(TP[:, 0:4], self.PRM[:, l, O_LLAM:O_LLAM + 4], AF.Exp, scale=-1.0)
            self.act(TP[:, 0:4], TP[:, 0:4], AF.Ln, bias=self.CST[:, 1:2])
            self.ts(self.DER[:, l, DR_S8:DR_S8 + 4], TP[:, 0:4], -4.0, None, ALU.mult)
            self.ts(self.DER[:, l, DR_S16:DR_S16 + 4], TP[:, 0:4], -8.0, None, ALU.mult)
            self.ts(self.DER[:, l, DR_HBR:DR_HBR + 4], self.PRM[:, l, O_LBR:O_LBR + 4], 0.5, None, ALU.mult)
            self.ts(self.DER[:, l, DR_HBI:DR_HBI + 4], self.PRM[:, l, O_LBI:O_LBI + 4], 0.5, None, ALU.mult)
            b = l * 256
            self.tt(TMP[:, 0:64], DL[:, b:b + 64], DL[:, b + 64:b + 128], ALU.mult)
            self.tt(TMP[:, 64:128], DL[:, b + 128:b + 192], DL[:, b + 192:b + 256], ALU.mult)
            S.op('dve', lambda e: e.reduce_sum(out=SS[:, 0:1], in_=TMP[:, 0:64], axis=AX.X),
                 reads=[TMP[:, 0:64]], writes=[SS[:, 0:1]])
            S.op('dve', lambda e: e.reduce_sum(out=SS[:, 1:2], in_=TMP[:, 64:128], axis=AX.X),
                 reads=[TMP[:, 64:128]], writes=[SS[:, 1:2]])
            self.act(SS[:, 2:4], SS[:, 0:2], AF.Exp)
            self.tt(LV[:, 0:1], SS[:, 2:3], SS[:, 3:4], ALU.subtract)
            self.ts(LV[:, 0:1], LV[:, 0:1], lam_init, None, ALU.add)
            self.ts(LV[:, 1:2], LV[:, 0:1], -1.0, None, ALU.mult)
            pl = self.pb()
            self.mm(pl[:, 0:2], ONES32[0:1, :], LV[0:1, 0:2])
            self.copy(self.DER[:, l, DR_LAM:DR_LAM + 2], pl[:, 0:2])
            self.ts(self.DER[:, l, DR_DSG:DR_DSG + 1], self.PRM[:, l, O_DSG:O_DSG + 1], 1.0 - lam_init, None, ALU.mult)

    def rmsnorm_begin(self, gcol_ap_fn):
        SQ = [self.alloc(f"nsq{i}", [128, T], BF16) for i in range(2)]
        RS = [self.alloc(f"nrs{i}", [128, T], F32) for i in range(1)]

        def norm_tile(t):
            tt = slice(t * T, (t + 1) * T)
            ps = self.pb()
            rs = RS[0]
            for c in range(KC):
                sq = SQ[c % 2]
                self.tt(sq[:], self.X[:, c, tt], self.X[:, c, tt], ALU.mult, eng='pool')
                self.mm(ps[:], self.ONESB[:], sq[:], start=(c == 0), stop=(c == KC - 1))
            self.act(rs[:], ps[:], AF.Ln, bias=self.CST[:, 0:1], scale=1.0 / D)
            self.act(rs[:], rs[:], AF.Exp, scale=-0.5)
            for c in range(KC):
                self.stt(self.H[:, c, tt], self.X[:, c, tt], gcol_ap_fn(c), rs[:], ALU.mult, ALU.mult)
        return norm_tile

    def dump(self, name, src_fn, nch):
        if name in self.dbg_out and self.cur_seq == 0:
            dst = self.dbg_out[name]
            self.S.dma('sp', 'dbg', [(dst[c * 128:(c + 1) * 128, :], src_fn(c)) for c in range(nch)])

    def merge_branch(self, l, j, Y, MERGED):
        NT = self.NT
        names = 'abc'[j]
        save = self.sc_off
        SG = [self.alloc(f"sg{i}", [128, T], F32) for i in range(4)]
        TM = self.alloc("mtmp", [128, T], F32)
        for half in range(2):
            wg = self.w_get(l, f'gate_{names}{half}')
            wout = self.w_get(l, f'out_{names}{half}', hold_prev=True)
            for mm_ in range(4):
                m = half * 4 + mm_
                for t in range(NT):
                    tt = slice(t * T, (t + 1) * T)
                    pg = self.pb()
                    for k in range(KC):
                        self.mm(pg[:], wg(k, mm_ * 128, (mm_ + 1) * 128), self.H[:, k, tt], start=(k == 0), stop=(k == KC - 1))
                    self.act(SG[t % 4][:], pg[:], AF.Sigmoid, bias=self.PRM[:, l, O_BGATE + j * 8 + m:O_BGATE + j * 8 + m + 1])
                for t in range(NT):
                    tt = slice(t * T, (t + 1) * T)
                    pp = self.pb()
                    sg = SG[t % 4]
                    for k in range(4):
                        self.mm(pp[:], wout(k, mm_ * 128, (mm_ + 1) * 128), Y(k, tt), start=(k == 0), stop=(k == 3))
                    if j == 0:
                        self.tt(MERGED[:, m, tt], sg[:], pp[:], ALU.mult)
                    else:
                        self.tt(TM[:], sg[:], pp[:], ALU.mult)
                        self.tt(MERGED[:, m, tt], MERGED[:, m, tt], TM[:], ALU.add)
        self.sc_off = save

    def proj_residual(self, l, names, SRC):
        NT = self.NT
        for half in range(2):
            w = self.w_get(l, names[half])
            for mm_ in range(4):
                m = half * 4 + mm_
                for t in range(NT):
                    tt = slice(t * T, (t + 1) * T)
                    ps = self.pb()
                    for k in range(KC):
                        self.mm(ps[:], w(k, mm_ * 128, (mm_ + 1) * 128), SRC[:, k, tt], start=(k == 0), stop=(k == KC - 1))
                    self.tt(self.X[:, m, tt], self.X[:, m, tt], ps[:], ALU.add)

    def layer(self, s, l):
        S = self.S
        NT, S_LEN = self.NT, self.S_LEN
        self.cur_seq = s
        P = lambda col: self.PRM[:, l, col:col + 1]
        DRV = lambda col: self.DER[:, l, col:col + 1]
        lbd = self.LBD[0]
        S.dma('sp', 'lbd0', [(lbd[:], self.wts[l, :, LBD_OFF:LBD_OFF + 1024])])

        self.sc_off = self.sc_base
        MERGED = self.alloc("MERGED", [128, KC, S_LEN], BF16)
        Y = self.alloc("Y", [128, 4, S_LEN], BF16)
        br_base = self.sc_off
        r1 = self.sc_base
        self.sc_off = br_base
        CBF = self.alloc("CBF", [128, 4, 30 + S_LEN], BF16)
        after_cbf = self.sc_off
        DG = self.alloc("DG", [128, 31, 128], BF16)
        norm_tile = self.rmsnorm_begin(lambda c: P(O_GMIX + c))
        small = KC * S_LEN * 2 < 32768
        if small:
            r1 = self.sc_off
        self.sc_off = r1
        ZX = self.alloc("ZX", [128, 3 + T], F32)
        XA0 = self.alloc("XA", [128, T], F32)
        RB = self.alloc("RB", [128, T], F32)
        IB = self.alloc("IB", [128, T], F32)
        AB = self.alloc("AB", [128, T], F32)
        GL = self.alloc("GL", [128, T], BF16)
        HS = [self.alloc(f"HS{i}", [128, T], F32) for i in range(2)]
        ACC = self.alloc("ACC", [128, 4, T], F32)
        SQB = [self.alloc(f"SQB{i}", [128, T], BF16) for i in range(2)]
        MEAN = self.alloc("MEAN", [128, T], F32)
        RSTD = self.alloc("RSTD", [128, T], F32)
        sgb_at = (self.sc_off + 63) // 64 * 64
        SGB = [self.alloc(f"SGB{i}", [128, T], BF16) for i in range(2)]
        XA1 = self.S.sbuf("XA1", [128, T], F32, sgb_at)
        XAs = [XA0, XA1]
        assert small or self.sc_off <= r1 + KC * S_LEN * 2, self.sc_off - r1

        wa = self.w_get(l, 'in_ca')
        wgc = self.w_get(l, 'in_cg', hold_prev=True)
        for c in range(4):
            self.memset(CBF[:, c, 0:30], 0.0)
        bcnt = 0
        norm_tile(0)
        for t in range(NT):
            tt = slice(t * T, (t + 1) * T)
            if t + 1 < NT:
                norm_tile(t + 1)
            for c in range(4):
                pa, pg = self.pb(), self.pb()
                for k in range(KC):
                    self.mm(pa[:], wa(k, c * 128, (c + 1) * 128), self.H[:, k, tt], start=(k == 0), stop=(k == KC - 1))
                for k in range(KC):
                    self.mm(pg[:], wgc(k, c * 128, (c + 1) * 128), self.H[:, k, tt], start=(k == 0), stop=(k == KC - 1))
                sgb = SGB[bcnt % 2]
                bcnt += 1
                self.act(sgb[:], pg[:], AF.Sigmoid)
                self.tt(CBF[:, c, 30 + t * T:30 + (t + 1) * T], sgb[:], pa[:], ALU.mult)

        wx = self.w_get(l, 'in_lx')
        wgl = self.w_get(l, 'in_lg', hold_prev=True)
        itsA = [(c, t) for c in range(4) for t in range(NT)]
        itsB = [(t, c) for t in range(NT) for c in range(4)]
        n_it = len(itsA)
        pcs = {}

        def B_build(i):
            t, c = itsB[i]
            for tap in range(31):
                if tap < 16:
                    self.ts(DG[:, tap, :], self.IDENT[:], P(O_CCW + tap * 4 + c), 0.0, ALU.mult, ALU.add, eng='pool')
                else:
                    self.ts(DG[:, tap, :], self.IDENT[:], P(O_CCW + tap * 4 + c), None, ALU.mult)

        def B_conv(i):
            t, c = itsB[i]
            pc = self.pb()
            for tap in range(31):
                self.mm(pc[:], DG[:, tap, :], CBF[:, c, t * T + tap:t * T + tap + T], start=(tap == 0), stop=(tap == 30))
            pcs[i] = pc

        def B_evac(i):
            t, c = itsB[i]
            pc = pcs.pop(i)
            self.act(ACC[:, c, :], pc[:], AF.Identity, bias=P(O_CCB + c))

        def B_ln(i):
            t, c = itsB[i]
            if c != 3:
                return
            pm, pq = self.pb(), self.pb()
            for c2 in range(4):
                self.act(SQB[0][:], ACC[:, c2, :], AF.Copy)
                self.mm(pm[:], self.ONESB[:], SQB[0][:], start=(c2 == 0), stop=(c2 == 3))
                self.act(SQB[1][:], ACC[:, c2, :], AF.Square)
                self.mm(pq[:], self.ONESB[:], SQB[1][:], start=(c2 == 0), stop=(c2 == 3))
            self.ts(MEAN[:], pm[:], 1.0 / 512, None, ALU.mult)
            self.tt(RSTD[:], MEAN[:], MEAN[:], ALU.mult)
            self.stt(RSTD[:], pq[:], 1.0 / 512, RSTD[:], ALU.mult, ALU.subtract)
            self.act(RSTD[:], RSTD[:], AF.Ln, bias=self.CST[:, 0:1])
            self.act(RSTD[:], RSTD[:], AF.Exp, scale=-0.5)
            for c2 in range(4):
                self.tt(ACC[:, c2, :], ACC[:, c2, :], MEAN[:], ALU.subtract)
                self.tt(ACC[:, c2, :], ACC[:, c2, :], RSTD[:], ALU.mult)
                self.act(CBF[:, c2, t * T:(t + 1) * T], ACC[:, c2, :], AF.Silu, bias=P(O_CLB + c2), scale=P(O_CLG + c2))

        pgs = {}

        def A_pre(i):
            c, t = itsA[i]
            XA = XAs[i % 2]
            tt = slice(t * T, (t + 1) * T)
            px, pg = self.pb(), self.pb()
            for k in range(KC):
                self.mm(px[:], wx(k, c * 128, (c + 1) * 128), self.H[:, k, tt], start=(k == 0), stop=(k == KC - 1))
            for k in range(KC):
                self.mm(pg[:], wgl(k, c * 128, (c + 1) * 128), self.H[:, k, tt], start=(k == 0), stop=(k == KC - 1))
            if t == 0:
                self.memset(ZX[:, 0:3], 0.0)
            else:
                self.copy(ZX[:, 0:3], ZX[:, T:T + 3])
            self.copy(ZX[:, 3:3 + T], px[:])
            self.ts(XA[:], ZX[:, 3:3 + T], P(O_LCW + 3 * 4 + c), P(O_LCB + c), ALU.mult, ALU.add)
            for tap in range(3):
                self.stt(XA[:], ZX[:, tap:tap + T], P(O_LCW + tap * 4 + c), XA[:], ALU.mult, ALU.add)
            pgs[i] = pg

        def A_mid(i):
            c, t = itsA[i]
            XA = XAs[i % 2]
            pg = pgs.pop(i)
            if i >= 1:
                B_conv(i - 1)
            pr, pi = self.pb(), self.pb()
            self.mm(pr[:], lbd[:, c * 128:(c + 1) * 128], XA[:])
            self.mm(pi[:], lbd[:, (4 + c) * 128:(5 + c) * 128], XA[:])
            if i >= 1:
                B_evac(i - 1)
            self.act(RB[:], pr[:], AF.Tanh, bias=DRV(DR_HBR + c), scale=0.5)
            self.act(IB[:], pi[:], AF.Tanh, bias=DRV(DR_HBI + c), scale=0.5)
            self.act(AB[:], RB[:], AF.Exp, bias=DRV(DR_S8 + c), scale=DRV(DR_S8 + c))
            self.act(RB[:], RB[:], AF.Exp, bias=DRV(DR_S16 + c), scale=DRV(DR_S16 + c))
            self.act(RB[:], RB[:], AF.Ln, bias=self.CST[:, 1:2], scale=-1.0)
            self.act(RB[:], RB[:], AF.Exp, scale=0.5)
            self.act(GL[:], pg[:], AF.Gelu_apprx_tanh)

        def A_tail(i):
            c, t = itsA[i]
            XA = XAs[i % 2]
            tt = slice(t * T, (t + 1) * T)
            self.stt(IB[:], IB[:], 1.0, XA[:], ALU.add, ALU.mult)
            self.stt(IB[:], IB[:], 0.5, RB[:], ALU.mult, ALU.mult)
            hs = HS[i % 2]
            init = 0.0 if t == 0 else HS[(i - 1) % 2][:, T - 1:T]
            self.scan(hs[:], AB[:], IB[:], init)
            self.tt(Y[:, c, tt], hs[:], GL[:], ALU.mult)

        A_pre(0)
        for i in range(n_it):
            A_mid(i)
            B_build(i)
            if i + 1 < n_it:
                A_pre(i + 1)
            if i >= 1:
                B_ln(i - 1)
            A_tail(i)
        B_conv(n_it - 1)
        B_evac(n_it - 1)
        B_ln(n_it - 1)
        self.dump(f'ya{l}', lambda c: Y[:, c, :], 4)
        self.dump(f'yb{l}', lambda c: CBF[:, c, 0:S_LEN], 4)
        self.sc_off = after_cbf
        self.merge_branch(l, 0, lambda k, tt: Y[:, k, tt], MERGED)
        self.sc_off = after_cbf
        self.merge_branch(l, 1, lambda k, tt: CBF[:, k, tt], MERGED)

        self.sc_off = br_base
        QT = self.alloc("QT", [128, S_LEN], BF16)
        KT2 = [self.alloc(f"KT{i}", [128, S_LEN], BF16) for i in range(2)]
        self.memset(KT2[0][64:128, :], 0.0)
        self.memset(KT2[1][0:64, :], 0.0)
        VV = self.alloc("VV", [128, S_LEN // 128, 128], BF16)
        PT = [self.alloc(f"PT{i}", [128, T], BF16) for i in range(4)]
        SB = [self.alloc(f"SBI{i}", [128, 256], F32) for i in range(2)]
        R0 = self.alloc("R0", [128, T], F32)
        R1 = self.alloc("R1", [128, T], F32)
        SQC = self.alloc("SQC", [128, T], BF16)
        bk = self.banks
        CBIAS = lambda hd: self.STRIP[:, hd, 255:256]
        for hd in range(4):
            wq = self.w_get(l, f'in_qkv{hd}')
            for t in range(NT):
                tt = slice(t * T, (t + 1) * T)
                pq, pk = self.pb(), self.pb()
                for k in range(KC):
                    self.mm(pq[:], wq(k, 0, 128), self.H[:, k, tt], start=(k == 0), stop=(k == KC - 1))
                for k in range(KC):
                    self.mm(pk[:], wq(k, 128, 256), self.H[:, k, tt], start=(k == 0), stop=(k == KC - 1))
                self.copy(QT[:, tt], pq[:])
                self.copy(KT2[0][0:64, tt], pk[0:64, :])
                self.copy(KT2[1][64:128, tt], pk[64:128, :])
            for qt in range(S_LEN // 128):
                pv = self.pb()
                for k in range(KC):
                    self.mm(pv[:, 0:128], self.H[:, k, qt * 128:(qt + 1) * 128], wq(k, 256, 384), start=(k == 0), stop=(k == KC - 1))
                self.copy(VV[:, qt, :], pv[:, 0:128])
            steps = [(G, c, K) for G in range(NT) for c in range(2) for K in range(4 * G + 4)]
            LOOK = 3

            def emit_S(i, hd=hd):
                G, c, K = steps[i]
                d = K - 4 * G
                q0 = max(d, 0) * 128
                ps = bk[4 + i % 4]
                pt = PT[i % 4]
                sb = SB[i % 2]
                self.mm(ps[:, q0:T], KT2[c][:, K * 128:(K + 1) * 128], QT[:, G * T + q0:(G + 1) * T])
                if d <= -2:
                    self.act(pt[:, q0:T], ps[:, q0:T], AF.Exp, bias=CBIAS(hd), scale=0.125)
                    return
                if d == -1:
                    s0, w = 128, 128
                else:
                    s0, w = 0, min(256, T - q0)
                self.stt(sb[:, 0:w], ps[:, q0:q0 + w], 0.125, self.STRIP[:, hd, s0:s0 + w], ALU.mult, ALU.add)
                self.act(pt[:, q0:q0 + w], sb[:, 0:w], AF.Exp)
                if q0 + w < T:
                    self.act(pt[:, q0 + w:T], ps[:, q0 + w:T], AF.Exp, bias=CBIAS(hd), scale=0.125)

            def emit_OD(i, hd=hd):
                G, c, K = steps[i]
                nK = 4 * G + 4
                d = K - 4 * G
                q0 = max(d, 0) * 128
                gi = G * 2 + c
                Ob, Db = bk[(gi % 2) * 2], bk[(gi % 2) * 2 + 1]
                pt = PT[i % 4]
                self.mm(Ob[:, q0:T], VV[:, K, :], pt[:, q0:T], start=(K == 0), stop=(K == nK - 1))
                self.mm(Db[:, q0:T], self.ONESB[:], pt[:, q0:T], start=(K == 0), stop=(K == nK - 1))
                if K != nK - 1:
                    return
                Rc = R0 if c == 0 else R1
                self.act(Rc[:], Db[:], AF.Ln)
                self.act(Rc[:], Rc[:], AF.Exp, scale=-1.0)
                self.tt(Rc[:], Ob[:], Rc[:], ALU.mult)
                if c == 0:
                    return
                gs = slice(G * T, (G + 1) * T)
                OOb = R0
                self.stt(OOb[:], R1[:], DRV(DR_NLAM), R0[:], ALU.mult, ALU.add)

                def part2(i=i, gs=gs, OOb=OOb, hd=hd):
                    self.act(SQC[:], OOb[:], AF.Square)
                    pn = bk[4 + i % 4]
                    self.mm(pn[:], self.ONESB[:], SQC[:])
                    self.act(R1[:], pn[:], AF.Ln, bias=self.CST[:, 0:1], scale=1.0 / 128)
                    self.act(R1[:], R1[:], AF.Exp, scale=-0.5)
                    self.stt(Y[:, hd, gs], OOb[:], DRV(DR_DSG), R1[:], ALU.mult, ALU.mult)
                deferred.append((i + 2, part2))

            deferred = []
            for i in range(len(steps) + LOOK):
                if i < len(steps):
                    emit_S(i)
                if i >= LOOK:
                    emit_OD(i - LOOK)
                    while deferred and deferred[0][0] <= i - LOOK:
                        deferred.pop(0)[1]()
            while deferred:
                deferred.pop(0)[1]()
        self.dump(f'yc{l}', lambda c: Y[:, c, :], 4)
        self.sc_off = br_base
        self.merge_branch(l, 2, lambda k, tt: Y[:, k, tt], MERGED)
        self.proj_residual(l, ('wo_0', 'wo_1'), MERGED)
        self.dump(f'x1_{l}', lambda c: self.X[:, c, :], 8)

        self.sc_off = self.sc_base
        MS = self.alloc("MS", [128, KC, NMEM], F32)
        MT = self.alloc("MT", [128, KC, NMEM], BF16)
        KX = self.alloc("KX", [128, KC, NMEM], BF16)
        XV = self.alloc("XV", [128, 2, D], BF16)
        QX = self.alloc("QX", [128, 2, S_LEN], BF16)
        OT = self.alloc("OT", [128, KC, S_LEN], BF16)
        PX = [self.alloc(f"PX{i}", [128, T], BF16) for i in range(4)]
        RD = [self.alloc(f"RD{i}", [128, T], F32) for i in range(2)]
        MSQ = [self.alloc(f"MSQ{i}", [128, NMEM], BF16) for i in range(2)]
        MRS = self.alloc("MRS", [128, NMEM], F32)
        S.dma('sp', 'memin', [(MS[:, c, :], self.memT[s, c * 128:(c + 1) * 128, :]) for c in range(KC)])
        ps = self.pb()
        for c in range(KC):
            self.act(MSQ[c % 2][:], MS[:, c, :], AF.Square)
            self.mm(ps[:, 0:NMEM], self.ONESB[:], MSQ[c % 2][:], start=(c == 0), stop=(c == KC - 1))
        self.act(MRS[:], ps[:, 0:NMEM], AF.Ln, bias=self.CST[:, 0:1], scale=1.0 / D)
        self.act(MRS[:], MRS[:], AF.Exp, scale=-0.5)
        for c in range(KC):
            self.stt(MT[:, c, :], MS[:, c, :], P(O_GMEM + c), MRS[:], ALU.mult, ALU.mult)
        for half in range(2):
            w = self.w_get(l, f'xk_{half}')
            for mm_ in range(4):
                m = half * 4 + mm_
                ps = self.pb()
                for k in range(KC):
                    self.mm(ps[:, 0:NMEM], w(k, mm_ * 128, (mm_ + 1) * 128), MT[:, k, :], start=(k == 0), stop=(k == KC - 1))
                self.copy(KX[:, m, :], ps[:, 0:NMEM])
        for half in range(2):
            w = self.w_get(l, f'xv_{half}')
            for mt in range(2):
                ps = self.pb()
                for k in range(KC):
                    self.mm(ps[:], MT[:, k, mt * 128:(mt + 1) * 128], w(k, 0, 512), start=(k == 0), stop=(k == KC - 1))
                self.act(XV[:, mt, half * 512:(half + 1) * 512], ps[:], AF.Copy)
        norm_tile = self.rmsnorm_begin(lambda c: P(O_GXA + c))
        def xa_S(hh, G, idx):
            gs = slice(G * T, (G + 1) * T)
            for kt in range(2):
                ps = self.pb()
                for cc in range(2):
                    self.mm(ps[:], KX[:, 2 * hh + cc, kt * 128:(kt + 1) * 128], QX[:, cc, gs], start=(cc == 0), stop=(cc == 1))
                self.act(PX[(idx % 2) * 2 + kt][:], ps[:], AF.Exp, scale=1.0 / 16)

        def xa_F(hh, G, idx):
            gs = slice(G * T, (G + 1) * T)
            pts = [PX[(idx % 2) * 2 + kt] for kt in range(2)]
            rd = RD[idx % 2]
            pd = self.pb()
            for kt in range(2):
                self.mm(pd[:], self.ONESB[:], pts[kt][:], start=(kt == 0), stop=(kt == 1))
            self.act(rd[:], pd[:], AF.Ln)
            self.act(rd[:], rd[:], AF.Exp, scale=-1.0)
            for ec in range(2):
                po = self.pb()
                for kt in range(2):
                    e0 = hh * 256 + ec * 128
                    self.mm(po[:], XV[:, kt, e0:e0 + 128], pts[kt][:], start=(kt == 0), stop=(kt == 1))
                self.tt(OT[:, 2 * hh + ec, gs], po[:], rd[:], ALU.mult)

        pending = None
        idx = 0
        for hh in range(4):
            w = self.w_get(l, f'xq{hh}')
            for cc in range(2):
                for t in range(NT):
                    tt = slice(t * T, (t + 1) * T)
                    if hh == 0 and cc == 0:
                        if t == 0:
                            norm_tile(0)
                        if t + 1 < NT:
                            norm_tile(t + 1)
                    ps = self.pb()
                    for k in range(KC):
                        self.mm(ps[:], w(k, cc * 128, (cc + 1) * 128), self.H[:, k, tt], start=(k == 0), stop=(k == KC - 1))
                    self.copy(QX[:, cc, tt], ps[:])
            for G in range(NT):
                xa_S(hh, G, idx)
                if pending is not None:
                    xa_F(*pending)
                pending = (hh, G, idx)
                idx += 1
        xa_F(*pending)
        self.proj_residual(l, ('xo_0', 'xo_1'), OT)
        self.dump(f'x2_{l}', lambda c: self.X[:, c, :], 8)

        self.sc_off = self.sc_base
        norm_tile = self.rmsnorm_begin(lambda c: P(O_GFFN + c))
        U = self.alloc("U", [128, 4, S_LEN], BF16)
        A2 = [self.alloc(f"A{i}", [128, 2 + T], F32) for i in range(2)]
        CV2 = [self.alloc(f"CV{i}", [128, T], F32) for i in range(2)]
        SL2 = [self.alloc(f"SL{i}", [128, T], F32) for i in range(2)]
        fcnt = 0
        HALO = self.alloc("HALO", [128, 4, 2], F32)
        for g, (c0, n) in enumerate(FGROUPS):
            w1 = self.w_get(l, f'w1_{g}')
            w3 = self.w_get(l, f'w3_{g}', hold_prev=True)
            for t in range(NT):
                tt = slice(t * T, (t + 1) * T)
                if g == 0:
                    if t == 0:
                        norm_tile(0)
                    if t + 1 < NT:
                        norm_tile(t + 1)
                for fi in range(n):
                    fc = c0 + fi
                    pa, p3 = self.pb(), self.pb()
                    for k in range(KC):
                        self.mm(pa[:], w1(k, fi * 128, (fi + 1) * 128), self.H[:, k, tt], start=(k == 0), stop=(k == KC - 1))
                    for k in range(KC):
                        self.mm(p3[:], w3(k, fi * 128, (fi + 1) * 128), self.H[:, k, tt], start=(k == 0), stop=(k == KC - 1))
                    A, CV, SL = A2[fcnt % 2], CV2[fcnt % 2], SL2[fcnt % 2]
                    fcnt += 1
                    if t == 0:
                        self.memset(A[:, 0:2], 0.0)
                    else:
                        self.copy(A[:, 0:2], HALO[:, fi, :])
                    self.act(A[:, 2:2 + T], pa[:], AF.Copy)
                    if t + 1 < NT:
                        self.copy(HALO[:, fi, :], A[:, T:T + 2])
                    self.ts(CV[:], A[:, 2:2 + T], P(O_FCW + 2 * FC + fc), P(O_FCB + fc), ALU.mult, ALU.add)
                    self.stt(CV[:], A[:, 1:1 + T], P(O_FCW + 1 * FC + fc), CV[:], ALU.mult, ALU.add)
                    self.stt(CV[:], A[:, 0:T], P(O_FCW + 0 * FC + fc), CV[:], ALU.mult, ALU.add)
                    self.act(SL[:], CV[:], AF.Silu)
                    self.tt(U[:, fi, tt], SL[:], p3[:], ALU.mult)
            w2 = self.w_get(l, f'w2_{g}')
            for t in range(NT):
                tt = slice(t * T, (t + 1) * T)
                for m in range(KC):
                    ps = self.pb()
                    for fi in range(n):
                        self.mm(ps[:], w2(fi, m * 128, (m + 1) * 128), U[:, fi, tt], start=(fi == 0), stop=(fi == n - 1))
                    self.tt(self.X[:, m, tt], self.X[:, m, tt], ps[:], ALU.add)
        self.dump(f'x3_{l}', lambda c: self.X[:, c, :], 8)

    def final(self, s):
        NT = self.NT
        self.sc_off = self.sc_base
        SQ = [self.alloc(f"fsq{i}", [128, T], BF16) for i in range(2)]
        RS = self.alloc("frs", [128, T], F32)
        OUTB = [self.alloc(f"OUTB{i}", [128, KC, T], F32) for i in range(2)]
        for t in range(NT):
            tt = slice(t * T, (t + 1) * T)
            ps = self.pb()
            for c in range(KC):
                sq = SQ[c % 2]
                self.act(sq[:], self.X[:, c, tt], AF.Square)
                self.mm(ps[:], self.ONESB[:], sq[:], start=(c == 0), stop=(c == KC - 1))
            self.act(RS[:], ps[:], AF.Ln, bias=self.CST[:, 0:1], scale=1.0 / D)
            self.act(RS[:], RS[:], AF.Exp, scale=-0.5)
            ob = OUTB[t % 2]
            for c in range(KC):
                self.stt(ob[:, c, :], self.X[:, c, tt], self.GFIN[:, c:c + 1], RS[:], ALU.mult, ALU.mult)
            self.S.dma('sp', f'out{t % 2}', [(self.outT[s, c * 128:(c + 1) * 128, tt], ob[:, c, :]) for c in range(KC)])


_CACHE = {}


def make_in_maps(inputs, n_cores, nseq, s_len, nl=NLAYER):
    inp = {k: np.asarray(v) for k, v in inputs.items()}
    wts = np.stack([pack_layer_weights(inp, l) for l in range(nl)])
    prm = np.stack([pack_layer_params(inp, l) for l in range(nl)])
    gfin = _cols(inp['final_g'])
    relb = np.ascontiguousarray(inp['rel_bias'], dtype=np.float32)
    oh = onehot_table()
    dalam = np.ascontiguousarray(inp['da_lambda'][:nl].reshape(1, nl * 256))
    maps = []
    for c in range(n_cores):
        xs = inp['x'][c * nseq:(c + 1) * nseq, :s_len]
        ms = inp['mem'][c * nseq:(c + 1) * nseq]
        maps.append({
            "xT": np.ascontiguousarray(xs.transpose(0, 2, 1)),
            "memT": np.ascontiguousarray(ms.transpose(0, 2, 1)),
            "wts": wts, "prm": prm, "gfin": gfin, "relb": relb, "onehot": oh, "dalam": dalam,
            "ident": np.eye(128, dtype=np.float32),
        })
    return maps


def kernel(**inputs):
    n = 8
    nseq = 2
    key = 'full'
    if key not in _CACHE:
        _CACHE[key] = Builder(2048, nseq, NLAYER).build()
    nc = _CACHE[key]
    maps = make_in_maps(inputs, n, nseq, 2048)
    res = run_bass_kernel_spmd(nc, maps, core_ids=list(range(n)))
    outs = [r["outT"].transpose(0, 2, 1) for r in res.results]
    return np.ascontiguousarray(np.concatenate(outs, axis=0).astype(np.float32))
```

```python
import math
from collections import defaultdict
from contextlib import ExitStack

import numpy as np
import concourse.bass as bass
import concourse.mybir as mybir
from concourse.bass_utils import run_bass_kernel_spmd

F32 = mybir.dt.float32
BF16 = mybir.dt.bfloat16
AF = mybir.ActivationFunctionType
ALU = mybir.AluOpType
AX = mybir.AxisListType
ESZ = {F32: 4, BF16: 2}
PAGE = 2048
SB_BASE = 16512
SB_END = 229344


class Sched:
    ENGS = ('pe', 'act', 'dve', 'pool', 'sp')

    def __init__(self, nc):
        self.nc = nc
        self.prog = {e: [] for e in self.ENGS}
        self.count = defaultdict(int)
        self.known = {e: defaultdict(int) for e in self.ENGS}
        self.snap = {}
        self.recs = defaultdict(lambda: defaultdict(list))
        self.tinfo = {}
        self.dma_keys = {}
        self.nwaits = 0
        self.nops = 0
        self.uid = 0

    def sbuf(self, name, shape, dtype, at):
        esz = ESZ[dtype]
        nbytes = int(np.prod(shape[1:])) * esz
        assert at >= SB_BASE and at + nbytes <= SB_END, (name, at, nbytes)
        self.uid += 1
        h = self.nc.alloc_sbuf_tensor_at(f"{name}_{self.uid}", list(shape), dtype, offset=at)
        self.tinfo[h.name] = ('sbuf', at, esz)
        return h

    def psum_banks(self):
        banks = []
        for i in range(8):
            h = self.nc.alloc_psum_tensor(f"psb{i}", [128, 512], F32)
            self.tinfo[h.name] = ('psum', i * 2048, 4)
            banks.append(h)
        return banks

    def dram(self, name, shape, dtype, kind="Internal"):
        h = self.nc.dram_tensor(name, list(shape), dtype, kind=kind)
        self.tinfo[h.name] = (None if kind == "ExternalInput" else 'dram:' + name, 0, ESZ[dtype])
        return h

    def _box(self, ap):
        space, base, esz = self.tinfo[ap.tensor.name]
        if space is None:
            return None
        pat = ap.ap
        off = int(ap.offset)
        if space.startswith('dram'):
            ext = sum((c - 1) * s for s, c in pat)
            return space, (0, 1, off * esz, (off + ext + 1) * esz)
        pcnt = pat[0][1]
        fsz = int(np.prod(ap.tensor.shape[1:]))
        p0 = off // fsz
        f0 = off % fsz
        if space == 'psum':
            return space, (0, 128, base, base + 2048)
        ext = sum((c - 1) * s for s, c in pat[1:])
        return space, (p0, p0 + pcnt, base + f0 * esz, base + (f0 + ext + 1) * esz)

    @staticmethod
    def _pages(space, box):
        if space.startswith('dram'):
            return (0,)
        return range(box[2] // PAGE, (box[3] - 1) // PAGE + 1)

    @staticmethod
    def _ovl(a, b):
        return a[0] < b[1] and b[0] < a[1] and a[2] < b[3] and b[2] < a[3]

    def _deps(self, reads, writes):
        deps = set()
        racc = [b for b in (self._box(ap) for ap in reads) if b is not None]
        wacc = [b for b in (self._box(ap) for ap in writes) if b is not None]
        wacc += [b for b in racc if b[0] == 'psum']
        racc = [b for b in racc if b[0] != 'psum']
        for space, box in racc:
            seen = set()
            sp = self.recs[space]
            for pg in self._pages(space, box):
                for r in sp[pg]:
                    if id(r) in seen:
                        continue
                    seen.add(id(r))
                    if r[1] and self._ovl(r[0], box):
                        deps.add((r[2], True))
        for space, box in wacc:
            seen = set()
            sp = self.recs[space]
            for pg in self._pages(space, box):
                for r in sp[pg]:
                    if id(r) in seen:
                        continue
                    seen.add(id(r))
                    if self._ovl(r[0], box):
                        deps.add((r[2], False))
        return deps, racc, wacc

    def _record(self, ev, racc, wacc):
        for space, box in wacc:
            pgs = self._pages(space, box)
            sp = self.recs[space]
            for pg in pgs:
                lst = sp[pg]
                lst[:] = [r for r in lst if not (box[0] <= r[0][0] and r[0][1] <= box[1]
                                                 and box[2] <= r[0][2] and r[0][3] <= box[3])]
            rec = (box, True, ev)
            for pg in pgs:
                sp[pg].append(rec)
        for space, box in racc:
            pgs = self._pages(space, box)
            sp = self.recs[space]
            rec = (box, False, ev)
            for pg in pgs:
                lst = sp[pg]
                lst[:] = [r for r in lst if not ((not r[1]) and r[0] == box and r[2][0] == ev[0])]
                lst.append(rec)

    def _emit_waits(self, eng, deps):
        kn = self.known[eng]
        need = {}
        for (key, n), raw in deps:
            if key == eng and eng == 'pe' and not raw:
                continue
            if kn[key] >= n:
                continue
            if need.get(key, 0) < n:
                need[key] = n
        for key, n in need.items():
            if kn[key] >= n:
                continue
            self.prog[eng].append(('wait', key, n))
            self.nwaits += 1
            sn = self.snap.get((key, n))
            if sn is not None:
                for k2, v2 in sn.items():
                    if kn[k2] < v2:
                        kn[k2] = v2
            if kn[key] < n:
                kn[key] = n

    def op(self, eng, fn, reads=(), writes=()):
        deps, racc, wacc = self._deps(reads, writes)
        self._emit_waits(eng, deps)
        self.count[eng] += 1
        ev = (eng, self.count[eng])
        self.snap[ev] = dict(self.known[eng])
        self.prog[eng].append(('op', fn))
        self._record(ev, racc, wacc)
        self.nops += 1
        return ev

    def dma(self, queue, key, pairs, **kw):
        dkey = 'dma:' + key
        if dkey not in self.dma_keys:
            self.dma_keys[dkey] = queue
        assert self.dma_keys[dkey] == queue
        prev = self.count[dkey]
        if prev and self.known[queue][dkey] < prev:
            self._emit_waits(queue, {((dkey, prev), True)})
        alld = set()
        accs = []
        for out, in_ in pairs:
            deps, racc, wacc = self._deps([in_], [out])
            alld |= deps
            accs.append((racc, wacc))
        self._emit_waits(queue, alld)
        self.count[dkey] += len(pairs)
        ev = (dkey, self.count[dkey])
        self.snap[ev] = dict(self.known[queue])
        for (out, in_), (racc, wacc) in zip(pairs, accs):
            self.prog[queue].append(('dma', out, in_, dkey, kw))
            self._record(ev, racc, wacc)
        return ev

    def finish(self, eng='sp'):
        deps = set()
        for dkey in self.dma_keys:
            if self.count[dkey]:
                deps.add(((dkey, self.count[dkey]), True))
        for e in self.ENGS:
            if e != eng and self.count[e]:
                deps.add(((e, self.count[e]), True))
        self._emit_waits(eng, deps)

    def emit(self):
        nc = self.nc
        keys = list(self.ENGS) + list(self.dma_keys.keys())
        with ExitStack() as st:
            sems = {}
            for k in keys:
                sems[k] = st.enter_context(nc.semaphore(k.replace(':', '_')))
            block = st.enter_context(nc.Block())
            engmap = {'pe': block.tensor, 'act': block.scalar, 'dve': block.vector,
                      'pool': block.gpsimd, 'sp': block.sync}
            for e in self.ENGS:
                prog = self.prog[e]
                esem = sems[e]

                def body(eobj, prog=prog, esem=esem):
                    for item in prog:
                        if item[0] == 'wait':
                            _, key, n = item
                            eobj.wait_ge(sems[key], n * (16 if key.startswith('dma:') else 1))
                        elif item[0] == 'op':
                            item[1](eobj).then_inc(esem, 1)
                        else:
                            _, out, in_, dkey, kw = item
                            eobj.dma_start(out=out, in_=in_, **kw).then_inc(sems[dkey], 16)
                engmap[e](body)


D = 1024
KC = 8
T = 512
NMEM = 256
DFF = 2816
FC = DFF // 128
NLAYER = 2
EPS = 1e-6
WT = 4096
NSLOT = 3

NPC = 320
O_GMIX, O_BGATE, O_LCW, O_LCB, O_LBR, O_LBI, O_LLAM = 0, 8, 32, 48, 52, 56, 60
O_CCW, O_CCB, O_CLG, O_CLB, O_DSG, O_GXA, O_GMEM, O_GFFN, O_FCW, O_FCB = 64, 188, 192, 196, 200, 201, 209, 217, 225, 291
DR_S8, DR_S16, DR_LAM, DR_NLAM, DR_DSG, DR_HBR, DR_HBI = 0, 4, 8, 9, 10, 12, 16

FGROUPS = [(g * 4, min(4, FC - g * 4)) for g in range((FC + 3) // 4)]


def tile_specs():
    sp = [('in_ca', 8, 512), ('in_cg', 8, 512), ('in_lx', 8, 512), ('in_lg', 8, 512),
          ('gate_a0', 8, 512), ('out_a0', 4, 512), ('gate_a1', 8, 512), ('out_a1', 4, 512),
          ('gate_b0', 8, 512), ('out_b0', 4, 512), ('gate_b1', 8, 512), ('out_b1', 4, 512)]
    sp += [(f'in_qkv{h}', 8, 384) for h in range(4)]
    sp += [('gate_c0', 8, 512), ('out_c0', 4, 512), ('gate_c1', 8, 512), ('out_c1', 4, 512),
           ('wo_0', 8, 512), ('wo_1', 8, 512),
           ('xk_0', 8, 512), ('xk_1', 8, 512), ('xv_0', 8, 512), ('xv_1', 8, 512)]
    sp += [(f'xq{h}', 8, 256) for h in range(4)]
    sp += [('xo_0', 8, 512), ('xo_1', 8, 512)]
    for g, (c0, n) in enumerate(FGROUPS):
        sp += [(f'w1_{g}', 8, n * 128), (f'w3_{g}', 8, n * 128), (f'w2_{g}', n, 1024)]
    return sp


TILE_SPECS = tile_specs()
TILE_OFF = {}
_o = 0
for _n, _k, _c in TILE_SPECS:
    TILE_OFF[_n] = (_o, _k, _c)
    _o += _k * _c
LBD_OFF = _o
_o += 1024
NPT = _o


def _kmajor(w):
    K, N = w.shape
    return np.ascontiguousarray(w.reshape(K // 128, 128, N).transpose(1, 0, 2))


def _cols(v):
    return np.ascontiguousarray(v.reshape(-1, 128).T)


def pack_layer_weights(inp, l):
    w_in = inp['w_in'][l]
    mats = {
        'in_lx': w_in[:, 0:512], 'in_lg': w_in[:, 512:1024],
        'out_a0': inp['lru_out'][l][:, 0:512], 'out_a1': inp['lru_out'][l][:, 512:1024],
        'out_b0': inp['cm_out'][l][:, 0:512], 'out_b1': inp['cm_out'][l][:, 512:1024],
        'out_c0': inp['da_out'][l][:, 0:512], 'out_c1': inp['da_out'][l][:, 512:1024],
        'gate_a0': inp['w_gate'][l, 0][:, 0:512], 'gate_a1': inp['w_gate'][l, 0][:, 512:1024],
        'in_ca': w_in[:, 1024:1536], 'in_cg': w_in[:, 1536:2048],
        'gate_b0': inp['w_gate'][l, 1][:, 0:512], 'gate_b1': inp['w_gate'][l, 1][:, 512:1024],
        'gate_c0': inp['w_gate'][l, 2][:, 0:512], 'gate_c1': inp['w_gate'][l, 2][:, 512:1024],
        'wo_0': inp['w_o'][l][:, 0:512], 'wo_1': inp['w_o'][l][:, 512:1024],
        'xk_0': inp['xa_wkv'][l][:, 0:512], 'xk_1': inp['xa_wkv'][l][:, 512:1024],
        'xv_0': inp['xa_wkv'][l][:, 1024:1536], 'xv_1': inp['xa_wkv'][l][:, 1536:2048],
        'xo_0': inp['xa_wo'][l][:, 0:512], 'xo_1': inp['xa_wo'][l][:, 512:1024],
    }
    for h in range(4):
        mats[f'in_qkv{h}'] = np.concatenate([w_in[:, 2048 + h * 128: 2048 + (h + 1) * 128],
                                             w_in[:, 2560 + h * 128: 2560 + (h + 1) * 128],
                                             w_in[:, 3072 + h * 128: 3072 + (h + 1) * 128]], axis=1)
        mats[f'xq{h}'] = inp['xa_wq'][l][:, h * 256:(h + 1) * 256]
    for g, (c0, n) in enumerate(FGROUPS):
        mats[f'w1_{g}'] = inp['ffn_w1'][l][:, c0 * 128:(c0 + n) * 128]
        mats[f'w3_{g}'] = inp['ffn_w3'][l][:, c0 * 128:(c0 + n) * 128]
        mats[f'w2_{g}'] = inp['ffn_w2'][l][c0 * 128:(c0 + n) * 128, :]
    out = np.zeros((128, NPT), np.float32)
    for name, kc, cols in TILE_SPECS:
        off = TILE_OFF[name][0]
        m = mats[name]
        assert m.shape == (kc * 128, cols), (name, m.shape)
        out[:, off:off + kc * cols] = _kmajor(m).reshape(128, kc * cols)
    bd = np.zeros((128, 8, 128), np.float32)
    for gi, key in enumerate(('lru_wr', 'lru_wi')):
        w = inp[key][l]
        for c in range(4):
            for hb in range(2):
                bd[hb * 64:(hb + 1) * 64, gi * 4 + c, hb * 64:(hb + 1) * 64] = w[2 * c + hb]
    out[:, LBD_OFF:LBD_OFF + 1024] = bd.reshape(128, 1024)
    return out


def pack_layer_params(inp, l):
    p = np.zeros((128, NPC), np.float32)
    p[:, O_GMIX:O_GMIX + 8] = _cols(inp['norm_mix_g'][l])
    for j in range(3):
        p[:, O_BGATE + j * 8:O_BGATE + (j + 1) * 8] = _cols(inp['b_gate'][l, j])
    for tap in range(4):
        p[:, O_LCW + tap * 4:O_LCW + (tap + 1) * 4] = _cols(inp['lru_conv_w'][l, tap])
    p[:, O_LCB:O_LCB + 4] = _cols(inp['lru_conv_b'][l])
    p[:, O_LBR:O_LBR + 4] = _cols(inp['lru_br'][l])
    p[:, O_LBI:O_LBI + 4] = _cols(inp['lru_bi'][l])
    p[:, O_LLAM:O_LLAM + 4] = _cols(inp['lru_lambda'][l])
    for tap in range(31):
        p[:, O_CCW + tap * 4:O_CCW + (tap + 1) * 4] = _cols(inp['cm_conv_w'][l, tap])
    p[:, O_CCB:O_CCB + 4] = _cols(inp['cm_conv_b'][l])
    p[:, O_CLG:O_CLG + 4] = _cols(inp['cm_ln_g'][l])
    p[:, O_CLB:O_CLB + 4] = _cols(inp['cm_ln_b'][l])
    p[:, O_DSG:O_DSG + 1] = _cols(inp['da_subln_g'][l])
    p[:, O_GXA:O_GXA + 8] = _cols(inp['norm_xa_g'][l])
    p[:, O_GMEM:O_GMEM + 8] = _cols(inp['norm_mem_g'][l])
    p[:, O_GFFN:O_GFFN + 8] = _cols(inp['norm_ffn_g'][l])
    for tap in range(3):
        p[:, O_FCW + tap * FC:O_FCW + (tap + 1) * FC] = _cols(inp['ffn_conv_w'][l, tap])
    p[:, O_FCB:O_FCB + FC] = _cols(inp['ffn_conv_b'][l])
    return p


def t5_bucket_np(rel):
    nb = 16
    ret = np.where(rel > 0, nb, 0)
    n = np.abs(rel)
    max_exact = 8
    lg = np.log(np.maximum(n, 1).astype(np.float32) / np.float32(max_exact)) / np.float32(math.log(128 / max_exact))
    large = max_exact + (lg.astype(np.float32) * np.float32(nb - max_exact)).astype(np.int32)
    large = np.minimum(large, nb - 1)
    return ret + np.where(n < max_exact, n, large)


def onehot_table():
    j = np.arange(768)
    b = t5_bucket_np(127 - j)
    oh = np.zeros((32, 768), np.float32)
    oh[b, j] = 1.0
    return oh


class Builder:
    def __init__(self, S_LEN=2048, NSEQ=2, NL=2, dbg=()):
        self.S_LEN, self.NSEQ, self.NL, self.dbg = S_LEN, NSEQ, NL, tuple(dbg)
        self.NT = S_LEN // T
        self.nc = bass.Bass("TRN2", target_bir_lowering=False)
        self.S = Sched(self.nc)
        self.bank_i = 0
        self.uid = 0

    def pb(self):
        b = self.banks[self.bank_i % 8]
        self.bank_i += 1
        return b

    @staticmethod
    def _aps(*xs):
        return [x for x in xs if isinstance(x, bass.AP)]

    def mm(self, out, lhsT, rhs, start=True, stop=True):
        self.S.op('pe', lambda e: e.matmul(out, lhsT=lhsT, rhs=rhs, start=start, stop=stop),
                  reads=[lhsT, rhs], writes=[out])

    def act(self, out, in_, func, bias=0.0, scale=1.0):
        self.S.op('act', lambda e: e.activation(out=out, in_=in_, func=func, bias=bias, scale=scale),
                  reads=self._aps(in_, bias, scale), writes=[out])

    def tt(self, out, in0, in1, op, eng='dve'):
        self.S.op(eng, lambda e: e.tensor_tensor(out=out, in0=in0, in1=in1, op=op),
                  reads=[in0, in1], writes=[out])

    def ts(self, out, in0, s1, s2, op0, op1=None, eng='dve'):
        if op1 is None:
            self.S.op(eng, lambda e: e.tensor_scalar(out=out, in0=in0, scalar1=s1, scalar2=None, op0=op0),
                      reads=self._aps(in0, s1), writes=[out])
        else:
            self.S.op(eng, lambda e: e.tensor_scalar(out=out, in0=in0, scalar1=s1, scalar2=s2, op0=op0, op1=op1),
                      reads=self._aps(in0, s1, s2), writes=[out])

    def stt(self, out, in0, scalar, in1, op0, op1):
        self.S.op('dve', lambda e: e.scalar_tensor_tensor(out=out, in0=in0, scalar=scalar, in1=in1, op0=op0, op1=op1),
                  reads=self._aps(in0, scalar, in1), writes=[out])

    def copy(self, out, in_, eng='dve'):
        self.S.op(eng, lambda e: e.tensor_copy(out=out, in_=in_), reads=[in_], writes=[out])

    def recip(self, out, in_):
        self.S.op('dve', lambda e: e.reciprocal(out=out, in_=in_), reads=[in_], writes=[out])

    def memset(self, ap, val, eng='dve'):
        self.S.op(eng, lambda e: e.memset(ap, val), reads=[], writes=[ap])

    def scan(self, out, d0, d1, initial):
        self.S.op('dve', lambda e: e.tensor_tensor_scan(out=out, data0=d0, data1=d1, initial=initial,
                                                        op0=ALU.mult, op1=ALU.add),
                  reads=self._aps(d0, d1, initial), writes=[out])

    def alloc(self, name, shape, dtype):
        nbytes = int(np.prod(shape[1:])) * ESZ[dtype]
        at = (self.sc_off + 63) // 64 * 64
        self.sc_off = at + nbytes
        assert self.sc_off <= SB_END, (name, self.sc_off)
        return self.S.sbuf(name, shape, dtype, at)

    def palloc(self, name, shape, dtype):
        nbytes = int(np.prod(shape[1:])) * ESZ[dtype]
        at = (self.p_off + 63) // 64 * 64
        self.p_off = at + nbytes
        return self.S.sbuf(name, shape, dtype, at)

    def w_init(self):
        self.w_sched = []
        for s in range(self.NSEQ):
            for l in range(self.NL):
                for name, kc, cols in TILE_SPECS:
                    self.w_sched.append((l, name, kc, cols))
        self.w_issued = 0
        self.w_next = 0

    def w_issue(self, upto):
        while self.w_issued <= min(upto, len(self.w_sched) - 1):
            i = self.w_issued
            l, name, kc, cols = self.w_sched[i]
            off = TILE_OFF[name][0]
            n = kc * cols
            slot = self.wslots[i % NSLOT]
            pairs = []
            for c0 in range(0, n, 2048):
                c1 = min(n, c0 + 2048)
                pairs.append((slot[:, c0:c1], self.wts[l, :, off + c0: off + c1]))
            self.S.dma('pool', f'w{i % NSLOT}', pairs)
            self.w_issued += 1

    def w_get(self, l, name, hold_prev=False):
        i = self.w_next
        ll, nm, kc, cols = self.w_sched[i]
        assert (ll, nm) == (l, name), (ll, nm, l, name)
        self.w_issue(i + (1 if hold_prev else 2))
        self.w_next += 1
        slot = self.wslots[i % NSLOT]

        def view(k, c0, c1):
            return slot[:, k * cols + c0: k * cols + c1]
        return view

    def build(self):
        S, nc = self.S, self.nc
        NSEQ, NL, S_LEN, NT = self.NSEQ, self.NL, self.S_LEN, self.NT
        self.xT = S.dram("xT", [NSEQ, D, S_LEN], F32, kind="ExternalInput").ap()
        self.memT = S.dram("memT", [NSEQ, D, NMEM], F32, kind="ExternalInput").ap()
        self.wts = S.dram("wts", [NL, 128, NPT], F32, kind="ExternalInput").ap()
        self.prm = S.dram("prm", [NL, 128, NPC], F32, kind="ExternalInput").ap()
        self.gfin = S.dram("gfin", [128, 8], F32, kind="ExternalInput").ap()
        self.relb = S.dram("relb", [32, 4], F32, kind="ExternalInput").ap()
        self.onehot = S.dram("onehot", [32, 768], F32, kind="ExternalInput").ap()
        self.dalam = S.dram("dalam", [1, NL * 256], F32, kind="ExternalInput").ap()
        self.ident = S.dram("ident", [128, 128], F32, kind="ExternalInput").ap()
        self.outT = S.dram("outT", [NSEQ, D, S_LEN], F32, kind="ExternalOutput").ap()
        self.d2h = S.dram("d2", [4, 128, 768], F32, kind="Internal")
        self.dbg_out = {}
        for name in self.dbg:
            if name.startswith('y'):
                self.dbg_out[name] = S.dram("dbg_" + name, [512, S_LEN], BF16, kind="ExternalOutput").ap()
            else:
                self.dbg_out[name] = S.dram("dbg_" + name, [D, S_LEN], F32, kind="ExternalOutput").ap()
        self.p_off = SB_BASE
        self.X = self.palloc("X", [128, KC, S_LEN], F32)
        self.H = self.palloc("H", [128, KC, S_LEN], BF16)
        self.PRM = self.palloc("PRM", [128, NL, NPC], F32)
        self.DER = self.palloc("DER", [128, NL, 32], F32)
        self.GFIN = self.palloc("GFIN", [128, 8], F32)
        self.ONESB = self.palloc("ONESB", [128, 128], BF16)
        self.CST = self.palloc("CST", [128, 4], F32)
        self.STRIP = self.palloc("STRIP", [128, 4, 256], F32)
        self.IDENT = self.palloc("IDENT", [128, 128], BF16)
        self.wslots = [self.palloc(f"WS{i}", [128, WT], BF16) for i in range(NSLOT)]
        self.LBD = [self.palloc(f"LBD{i}", [128, 1024], F32) for i in range(1)]
        self.sc_base = (self.p_off + 63) // 64 * 64
        self.sc_off = self.sc_base
        self.banks = S.psum_banks()
        self.w_init()

        def load_x(s):
            for t in range(NT):
                tt = slice(t * T, (t + 1) * T)
                S.dma('sp', f'xin{t}', [(self.X[:, c, tt], self.xT[s, c * 128:(c + 1) * 128, tt]) for c in range(KC)])

        load_x(0)
        self.setup()
        for s in range(NSEQ):
            if s > 0:
                load_x(s)
            for l in range(NL):
                self.layer(s, l)
            self.final(s)
        S.finish('sp')
        S.emit()
        return nc

    def setup(self):
        S = self.S
        NL = self.NL
        self.sc_off = self.sc_base
        S.dma('sp', 'cst', [(self.PRM[:, l, :], self.prm[l]) for l in range(NL)] + [(self.GFIN[:], self.gfin)])
        S.dma('pool', 'identld', [(self.IDENT[:], self.ident)])
        self.memset(self.ONESB[:], 1.0)
        self.memset(self.CST[:, 0:1], EPS)
        self.memset(self.CST[:, 1:2], 1.0)
        TB = self.alloc("TB", [32, 4], F32)
        OH = self.alloc("OH", [32, 768], F32)
        ONES32 = self.alloc("ONES32", [32, 128], F32)
        LH = self.alloc("LH", [32, 128], F32)
        GB = self.alloc("GB", [128, 768], F32)
        DL = self.alloc("DL", [1, NL * 256], F32)
        TMP = self.alloc("TMPL", [1, 128], F32)
        SS = self.alloc("SSL", [1, 4], F32)
        LV = self.alloc("LV", [1, 2], F32)
        TP = self.alloc("TPS", [128, 4], F32)
        S.dma('sp', 'cst', [(TB[:], self.relb), (OH[:], self.onehot), (DL[:], self.dalam)])
        self.memset(ONES32[:], 1.0)
        d2 = self.d2h.ap()
        for h in range(4):
            self.ts(LH[:], ONES32[:], TB[:, h:h + 1], None, ALU.mult)
            pa, pb2 = self.pb(), self.pb()
            self.mm(pa[:, 0:512], LH[:], OH[:, 0:512])
            self.mm(pb2[:, 0:256], LH[:], OH[:, 512:768])
            self.copy(GB[:, 0:512], pa[:, 0:512])
            self.copy(GB[:, 512:768], pb2[:, 0:256])
            S.dma('sp', 'strip', [(d2[h], GB[:])])
            skew = bass.AP(self.d2h, h * 128 * 768 + 127, [[767, 128], [1, 256]])
            S.dma('sp', 'strip', [(self.STRIP[:, h, :], skew)])
            self.memset(self.STRIP[64:128, h, 0:64], -30000.0)
        for l in range(NL):
            lam_init = 0.8 - 0.6 * math.exp(-0.3 * l)
            self.act(TP[:, 0:4], self.PRM[:, l, O_LLAM:O_LLAM + 4], AF.Exp, scale=-1.0)
            self.act(TP[:, 0:4], TP[:, 0:4], AF.Ln, bias=self.CST[:, 1:2])
            self.ts(self.DER[:, l, DR_S8:DR_S8 + 4], TP[:, 0:4], -4.0, None, ALU.mult)
            self.ts(self.DER[:, l, DR_S16:DR_S16 + 4], TP[:, 0:4], -8.0, None, ALU.mult)
            self.ts(self.DER[:, l, DR_HBR:DR_HBR + 4], self.PRM[:, l, O_LBR:O_LBR + 4], 0.5, None, ALU.mult)
            self.ts(self.DER[:, l, DR_HBI:DR_HBI + 4], self.PRM[:, l, O_LBI:O_LBI + 4], 0.5, None, ALU.mult)
            b = l * 256
            self.tt(TMP[:, 0:64], DL[:, b:b + 64], DL[:, b + 64:b + 128], ALU.mult)
            self.tt(TMP[:, 64:128], DL[:, b + 128:b + 192], DL[:, b + 192:b + 256], ALU.mult)
            S.op('dve', lambda e: e.reduce_sum(out=SS[:, 0:1], in_=TMP[:, 0:64], axis=AX.X),
                 reads=[TMP[:, 0:64]], writes=[SS[:, 0:1]])
            S.op('dve', lambda e: e.reduce_sum(out=SS[:, 1:2], in_=TMP[:, 64:128], axis=AX.X),
                 reads=[TMP[:, 64:128]], writes=[SS[:, 1:2]])
            self.act(SS[:, 2:4], SS[:, 0:2], AF.Exp)
            self.tt(LV[:, 0:1], SS[:, 2:3], SS[:, 3:4], ALU.subtract)
            self.ts(LV[:, 0:1], LV[:, 0:1], lam_init, None, ALU.add)
            self.ts(LV[:, 1:2], LV[:, 0:1], -1.0, None, ALU.mult)
            pl = self.pb()
            self.mm(pl[:, 0:2], ONES32[0:1, :], LV[0:1, 0:2])
            self.copy(self.DER[:, l, DR_LAM:DR_LAM + 2], pl[:, 0:2])
            self.ts(self.DER[:, l, DR_DSG:DR_DSG + 1], self.PRM[:, l, O_DSG:O_DSG + 1], 1.0 - lam_init, None, ALU.mult)

    def rmsnorm_begin(self, gcol_ap_fn):
        SQ = [self.alloc(f"nsq{i}", [128, T], BF16) for i in range(2)]
        RS = [self.alloc(f"nrs{i}", [128, T], F32) for i in range(1)]

        def norm_tile(t):
            tt = slice(t * T, (t + 1) * T)
            ps = self.pb()
            rs = RS[0]
            for c in range(KC):
                sq = SQ[c % 2]
                if c % 2 == 0:
                    self.tt(sq[:], self.X[:, c, tt], self.X[:, c, tt], ALU.mult, eng='pool')
                else:
                    self.act(sq[:], self.X[:, c, tt], AF.Square)
                self.mm(ps[:], self.ONESB[:], sq[:], start=(c == 0), stop=(c == KC - 1))
            self.act(rs[:], ps[:], AF.Ln, bias=self.CST[:, 0:1], scale=1.0 / D)
            self.act(rs[:], rs[:], AF.Exp, scale=-0.5)
            for c in range(KC):
                self.stt(self.H[:, c, tt], self.X[:, c, tt], gcol_ap_fn(c), rs[:], ALU.mult, ALU.mult)
        return norm_tile

    def dump(self, name, src_fn, nch):
        if name in self.dbg_out and self.cur_seq == 0:
            dst = self.dbg_out[name]
            self.S.dma('sp', 'dbg', [(dst[c * 128:(c + 1) * 128, :], src_fn(c)) for c in range(nch)])

    def merge_branch(self, l, j, Y, MERGED):
        NT = self.NT
        names = 'abc'[j]
        save = self.sc_off
        SG = [self.alloc(f"sg{i}", [128, T], F32) for i in range(4)]
        TM = self.alloc("mtmp", [128, T], F32)
        for half in range(2):
            wg = self.w_get(l, f'gate_{names}{half}')
            wout = self.w_get(l, f'out_{names}{half}', hold_prev=True)
            for mm_ in range(4):
                m = half * 4 + mm_
                for t in range(NT):
                    tt = slice(t * T, (t + 1) * T)
                    pg = self.pb()
                    for k in range(KC):
                        self.mm(pg[:], wg(k, mm_ * 128, (mm_ + 1) * 128), self.H[:, k, tt], start=(k == 0), stop=(k == KC - 1))
                    self.act(SG[t % 4][:], pg[:], AF.Sigmoid, bias=self.PRM[:, l, O_BGATE + j * 8 + m:O_BGATE + j * 8 + m + 1])
                for t in range(NT):
                    tt = slice(t * T, (t + 1) * T)
                    pp = self.pb()
                    sg = SG[t % 4]
                    for k in range(4):
                        self.mm(pp[:], wout(k, mm_ * 128, (mm_ + 1) * 128), Y(k, tt), start=(k == 0), stop=(k == 3))
                    if j == 0:
                        self.tt(MERGED[:, m, tt], sg[:], pp[:], ALU.mult)
                    else:
                        self.tt(TM[:], sg[:], pp[:], ALU.mult)
                        self.tt(MERGED[:, m, tt], MERGED[:, m, tt], TM[:], ALU.add)
        self.sc_off = save

    def proj_residual(self, l, names, SRC):
        NT = self.NT
        for half in range(2):
            w = self.w_get(l, names[half])
            for mm_ in range(4):
                m = half * 4 + mm_
                for t in range(NT):
                    tt = slice(t * T, (t + 1) * T)
                    ps = self.pb()
                    for k in range(KC):
                        self.mm(ps[:], w(k, mm_ * 128, (mm_ + 1) * 128), SRC[:, k, tt], start=(k == 0), stop=(k == KC - 1))
                    self.tt(self.X[:, m, tt], self.X[:, m, tt], ps[:], ALU.add)

    def layer(self, s, l):
        S = self.S
        NT, S_LEN = self.NT, self.S_LEN
        self.cur_seq = s
        P = lambda col: self.PRM[:, l, col:col + 1]
        DRV = lambda col: self.DER[:, l, col:col + 1]
        lbd = self.LBD[0]
        S.dma('sp', 'lbd0', [(lbd[:], self.wts[l, :, LBD_OFF:LBD_OFF + 1024])])

        self.sc_off = self.sc_base
        MERGED = self.alloc("MERGED", [128, KC, S_LEN], BF16)
        Y = self.alloc("Y", [128, 4, S_LEN], BF16)
        br_base = self.sc_off
        r1 = self.sc_base
        self.sc_off = br_base
        CBF = self.alloc("CBF", [128, 4, 30 + S_LEN], BF16)
        after_cbf = self.sc_off
        DG = self.alloc("DG", [128, 31, 128], BF16)
        norm_tile = self.rmsnorm_begin(lambda c: P(O_GMIX + c))
        small = KC * S_LEN * 2 < 32768
        if small:
            r1 = self.sc_off
        self.sc_off = r1
        ZX = self.alloc("ZX", [128, 3 + T], F32)
        XA0 = self.alloc("XA", [128, T], F32)
        RB = self.alloc("RB", [128, T], F32)
        IB = self.alloc("IB", [128, T], F32)
        AB = self.alloc("AB", [128, T], F32)
        GL = self.alloc("GL", [128, T], BF16)
        HS = [self.alloc(f"HS{i}", [128, T], F32) for i in range(2)]
        ACC = self.alloc("ACC", [128, 4, T], F32)
        SQB = [self.alloc(f"SQB{i}", [128, T], BF16) for i in range(2)]
        MEAN = self.alloc("MEAN", [128, T], F32)
        RSTD = self.alloc("RSTD", [128, T], F32)
        sgb_at = (self.sc_off + 63) // 64 * 64
        SGB = [self.alloc(f"SGB{i}", [128, T], BF16) for i in range(2)]
        XA1 = self.S.sbuf("XA1", [128, T], F32, sgb_at)
        XAs = [XA0, XA1]
        assert small or self.sc_off <= r1 + KC * S_LEN * 2, self.sc_off - r1

        wa = self.w_get(l, 'in_ca')
        wgc = self.w_get(l, 'in_cg', hold_prev=True)
        for c in range(4):
            self.memset(CBF[:, c, 0:30], 0.0)
        bcnt = 0
        norm_tile(0)
        for t in range(NT):
            tt = slice(t * T, (t + 1) * T)
            if t + 1 < NT:
                norm_tile(t + 1)
            for c in range(4):
                pa, pg = self.pb(), self.pb()
                for k in range(KC):
                    self.mm(pa[:], wa(k, c * 128, (c + 1) * 128), self.H[:, k, tt], start=(k == 0), stop=(k == KC - 1))
                for k in range(KC):
                    self.mm(pg[:], wgc(k, c * 128, (c + 1) * 128), self.H[:, k, tt], start=(k == 0), stop=(k == KC - 1))
                sgb = SGB[bcnt % 2]
                bcnt += 1
                self.act(sgb[:], pg[:], AF.Sigmoid)
                self.tt(CBF[:, c, 30 + t * T:30 + (t + 1) * T], sgb[:], pa[:], ALU.mult)

        wx = self.w_get(l, 'in_lx')
        wgl = self.w_get(l, 'in_lg', hold_prev=True)
        itsA = [(c, t) for c in range(4) for t in range(NT)]
        itsB = [(t, c) for t in range(NT) for c in range(4)]
        n_it = len(itsA)
        pcs = {}

        def B_build(i):
            t, c = itsB[i]
            for tap in range(31):
                if tap < 16:
                    self.ts(DG[:, tap, :], self.IDENT[:], P(O_CCW + tap * 4 + c), 0.0, ALU.mult, ALU.add, eng='pool')
                else:
                    self.ts(DG[:, tap, :], self.IDENT[:], P(O_CCW + tap * 4 + c), None, ALU.mult)

        def B_conv(i):
            t, c = itsB[i]
            pc = self.pb()
            for tap in range(31):
                self.mm(pc[:], DG[:, tap, :], CBF[:, c, t * T + tap:t * T + tap + T], start=(tap == 0), stop=(tap == 30))
            pcs[i] = pc

        def B_evac(i):
            t, c = itsB[i]
            pc = pcs.pop(i)
            self.act(ACC[:, c, :], pc[:], AF.Identity, bias=P(O_CCB + c))

        def B_ln(i):
            t, c = itsB[i]
            if c != 3:
                return
            pm, pq = self.pb(), self.pb()
            for c2 in range(4):
                self.act(SQB[0][:], ACC[:, c2, :], AF.Copy)
                self.mm(pm[:], self.ONESB[:], SQB[0][:], start=(c2 == 0), stop=(c2 == 3))
                self.act(SQB[1][:], ACC[:, c2, :], AF.Square)
                self.mm(pq[:], self.ONESB[:], SQB[1][:], start=(c2 == 0), stop=(c2 == 3))
            self.ts(MEAN[:], pm[:], 1.0 / 512, None, ALU.mult)
            self.tt(RSTD[:], MEAN[:], MEAN[:], ALU.mult)
            self.stt(RSTD[:], pq[:], 1.0 / 512, RSTD[:], ALU.mult, ALU.subtract)
            self.act(RSTD[:], RSTD[:], AF.Ln, bias=self.CST[:, 0:1])
            self.act(RSTD[:], RSTD[:], AF.Exp, scale=-0.5)
            for c2 in range(4):
                self.tt(ACC[:, c2, :], ACC[:, c2, :], MEAN[:], ALU.subtract)
                self.tt(ACC[:, c2, :], ACC[:, c2, :], RSTD[:], ALU.mult)
                self.act(CBF[:, c2, t * T:(t + 1) * T], ACC[:, c2, :], AF.Silu, bias=P(O_CLB + c2), scale=P(O_CLG + c2))

        pgs = {}

        def A_pre(i):
            c, t = itsA[i]
            XA = XAs[i % 2]
            tt = slice(t * T, (t + 1) * T)
            px, pg = self.pb(), self.pb()
            for k in range(KC):
                self.mm(px[:], wx(k, c * 128, (c + 1) * 128), self.H[:, k, tt], start=(k == 0), stop=(k == KC - 1))
            for k in range(KC):
                self.mm(pg[:], wgl(k, c * 128, (c + 1) * 128), self.H[:, k, tt], start=(k == 0), stop=(k == KC - 1))
            if t == 0:
                self.memset(ZX[:, 0:3], 0.0)
            else:
                self.copy(ZX[:, 0:3], ZX[:, T:T + 3])
            self.copy(ZX[:, 3:3 + T], px[:])
            self.ts(XA[:], ZX[:, 3:3 + T], P(O_LCW + 3 * 4 + c), P(O_LCB + c), ALU.mult, ALU.add)
            for tap in range(3):
                self.stt(XA[:], ZX[:, tap:tap + T], P(O_LCW + tap * 4 + c), XA[:], ALU.mult, ALU.add)
            pgs[i] = pg

        def A_mid(i):
            c, t = itsA[i]
            XA = XAs[i % 2]
            pg = pgs.pop(i)
            if i >= 1:
                B_conv(i - 1)
            pr, pi = self.pb(), self.pb()
            self.mm(pr[:], lbd[:, c * 128:(c + 1) * 128], XA[:])
            self.mm(pi[:], lbd[:, (4 + c) * 128:(5 + c) * 128], XA[:])
            if i >= 1:
                B_evac(i - 1)
            self.act(RB[:], pr[:], AF.Tanh, bias=DRV(DR_HBR + c), scale=0.5)
            self.act(IB[:], pi[:], AF.Tanh, bias=DRV(DR_HBI + c), scale=0.5)
            self.act(AB[:], RB[:], AF.Exp, bias=DRV(DR_S8 + c), scale=DRV(DR_S8 + c))
            self.act(RB[:], RB[:], AF.Exp, bias=DRV(DR_S16 + c), scale=DRV(DR_S16 + c))
            self.act(RB[:], RB[:], AF.Ln, bias=self.CST[:, 1:2], scale=-1.0)
            self.act(RB[:], RB[:], AF.Exp, scale=0.5)
            self.act(GL[:], pg[:], AF.Gelu_apprx_tanh)

        def A_tail(i):
            c, t = itsA[i]
            XA = XAs[i % 2]
            tt = slice(t * T, (t + 1) * T)
            self.stt(IB[:], IB[:], 1.0, XA[:], ALU.add, ALU.mult)
            self.stt(IB[:], IB[:], 0.5, RB[:], ALU.mult, ALU.mult)
            hs = HS[i % 2]
            init = 0.0 if t == 0 else HS[(i - 1) % 2][:, T - 1:T]
            self.scan(hs[:], AB[:], IB[:], init)
            self.tt(Y[:, c, tt], hs[:], GL[:], ALU.mult)

        A_pre(0)
        for i in range(n_it):
            A_mid(i)
            B_build(i)
            if i + 1 < n_it:
                A_pre(i + 1)
            if i >= 1:
                B_ln(i - 1)
            A_tail(i)
        B_conv(n_it - 1)
        B_evac(n_it - 1)
        B_ln(n_it - 1)
        self.dump(f'ya{l}', lambda c: Y[:, c, :], 4)
        self.dump(f'yb{l}', lambda c: CBF[:, c, 0:S_LEN], 4)
        self.sc_off = after_cbf
        self.merge_branch(l, 0, lambda k, tt: Y[:, k, tt], MERGED)
        self.sc_off = after_cbf
        self.merge_branch(l, 1, lambda k, tt: CBF[:, k, tt], MERGED)

        self.sc_off = br_base
        QT = self.alloc("QT", [128, S_LEN], BF16)
        KT2 = [self.alloc(f"KT{i}", [128, S_LEN], BF16) for i in range(2)]
        self.memset(KT2[0][64:128, :], 0.0)
        self.memset(KT2[1][0:64, :], 0.0)
        VV = self.alloc("VV", [128, S_LEN // 128, 128], BF16)
        PT = [self.alloc(f"PT{i}", [128, T], BF16) for i in range(4)]
        SB = [self.alloc(f"SBI{i}", [128, 256], F32) for i in range(2)]
        R0 = self.alloc("R0", [128, T], F32)
        R1 = self.alloc("R1", [128, T], F32)
        SQC = self.alloc("SQC", [128, T], BF16)
        bk = self.banks
        CBIAS = lambda hd: self.STRIP[:, hd, 255:256]
        for hd in range(4):
            wq = self.w_get(l, f'in_qkv{hd}')
            for t in range(NT):
                tt = slice(t * T, (t + 1) * T)
                pq, pk = self.pb(), self.pb()
                for k in range(KC):
                    self.mm(pq[:], wq(k, 0, 128), self.H[:, k, tt], start=(k == 0), stop=(k == KC - 1))
                for k in range(KC):
                    self.mm(pk[:], wq(k, 128, 256), self.H[:, k, tt], start=(k == 0), stop=(k == KC - 1))
                self.copy(QT[:, tt], pq[:])
                self.copy(KT2[0][0:64, tt], pk[0:64, :])
                self.copy(KT2[1][64:128, tt], pk[64:128, :])
            for qt in range(S_LEN // 128):
                pv = self.pb()
                for k in range(KC):
                    self.mm(pv[:, 0:128], self.H[:, k, qt * 128:(qt + 1) * 128], wq(k, 256, 384), start=(k == 0), stop=(k == KC - 1))
                self.copy(VV[:, qt, :], pv[:, 0:128])
            steps = [(G, c, K) for G in range(NT) for c in range(2) for K in range(4 * G + 4)]
            LOOK = 3

            def emit_S(i, hd=hd):
                G, c, K = steps[i]
                d = K - 4 * G
                q0 = max(d, 0) * 128
                ps = bk[4 + i % 4]
                pt = PT[i % 4]
                sb = SB[i % 2]
                self.mm(ps[:, q0:T], KT2[c][:, K * 128:(K + 1) * 128], QT[:, G * T + q0:(G + 1) * T])
                if d <= -2:
                    self.act(pt[:, q0:T], ps[:, q0:T], AF.Exp, bias=CBIAS(hd), scale=0.125)
                    return
                if d == -1:
                    s0, w = 128, 128
                else:
                    s0, w = 0, min(256, T - q0)
                self.stt(sb[:, 0:w], ps[:, q0:q0 + w], 0.125, self.STRIP[:, hd, s0:s0 + w], ALU.mult, ALU.add)
                self.act(pt[:, q0:q0 + w], sb[:, 0:w], AF.Exp)
                if q0 + w < T:
                    self.act(pt[:, q0 + w:T], ps[:, q0 + w:T], AF.Exp, bias=CBIAS(hd), scale=0.125)

            def emit_OD(i, hd=hd):
                G, c, K = steps[i]
                nK = 4 * G + 4
                d = K - 4 * G
                q0 = max(d, 0) * 128
                gi = G * 2 + c
                Ob, Db = bk[(gi % 2) * 2], bk[(gi % 2) * 2 + 1]
                pt = PT[i % 4]
                self.mm(Ob[:, q0:T], VV[:, K, :], pt[:, q0:T], start=(K == 0), stop=(K == nK - 1))
                self.mm(Db[:, q0:T], self.ONESB[:], pt[:, q0:T], start=(K == 0), stop=(K == nK - 1))
                if K != nK - 1:
                    return
                Rc = R0 if c == 0 else R1
                self.act(Rc[:], Db[:], AF.Ln)
                self.act(Rc[:], Rc[:], AF.Exp, scale=-1.0)
                self.tt(Rc[:], Ob[:], Rc[:], ALU.mult)
                if c == 0:
                    return
                gs = slice(G * T, (G + 1) * T)
                OOb = R0
                self.stt(OOb[:], R1[:], DRV(DR_NLAM), R0[:], ALU.mult, ALU.add)

                def part2(i=i, gs=gs, OOb=OOb, hd=hd):
                    self.act(SQC[:], OOb[:], AF.Square)
                    pn = bk[4 + i % 4]
                    self.mm(pn[:], self.ONESB[:], SQC[:])
                    self.act(R1[:], pn[:], AF.Ln, bias=self.CST[:, 0:1], scale=1.0 / 128)
                    self.act(R1[:], R1[:], AF.Exp, scale=-0.5)
                    self.stt(Y[:, hd, gs], OOb[:], DRV(DR_DSG), R1[:], ALU.mult, ALU.mult)
                deferred.append((i + 2, part2))

            deferred = []
            for i in range(len(steps) + LOOK):
                if i < len(steps):
                    emit_S(i)
                if i >= LOOK:
                    emit_OD(i - LOOK)
                    while deferred and deferred[0][0] <= i - LOOK:
                        deferred.pop(0)[1]()
            while deferred:
                deferred.pop(0)[1]()
        self.dump(f'yc{l}', lambda c: Y[:, c, :], 4)
        self.sc_off = br_base
        self.merge_branch(l, 2, lambda k, tt: Y[:, k, tt], MERGED)
        self.proj_residual(l, ('wo_0', 'wo_1'), MERGED)
        self.dump(f'x1_{l}', lambda c: self.X[:, c, :], 8)

        self.sc_off = self.sc_base
        MS = self.alloc("MS", [128, KC, NMEM], F32)
        MT = self.alloc("MT", [128, KC, NMEM], BF16)
        KX = self.alloc("KX", [128, KC, NMEM], BF16)
        XV = self.alloc("XV", [128, 2, D], BF16)
        QX = self.alloc("QX", [128, 2, S_LEN], BF16)
        OT = self.alloc("OT", [128, KC, S_LEN], BF16)
        PX = [self.alloc(f"PX{i}", [128, T], BF16) for i in range(4)]
        RD = [self.alloc(f"RD{i}", [128, T], F32) for i in range(2)]
        MSQ = [self.alloc(f"MSQ{i}", [128, NMEM], BF16) for i in range(2)]
        MRS = self.alloc("MRS", [128, NMEM], F32)
        S.dma('sp', 'memin', [(MS[:, c, :], self.memT[s, c * 128:(c + 1) * 128, :]) for c in range(KC)])
        ps = self.pb()
        for c in range(KC):
            self.act(MSQ[c % 2][:], MS[:, c, :], AF.Square)
            self.mm(ps[:, 0:NMEM], self.ONESB[:], MSQ[c % 2][:], start=(c == 0), stop=(c == KC - 1))
        self.act(MRS[:], ps[:, 0:NMEM], AF.Ln, bias=self.CST[:, 0:1], scale=1.0 / D)
        self.act(MRS[:], MRS[:], AF.Exp, scale=-0.5)
        for c in range(KC):
            self.stt(MT[:, c, :], MS[:, c, :], P(O_GMEM + c), MRS[:], ALU.mult, ALU.mult)
        for half in range(2):
            w = self.w_get(l, f'xk_{half}')
            for mm_ in range(4):
                m = half * 4 + mm_
                ps = self.pb()
                for k in range(KC):
                    self.mm(ps[:, 0:NMEM], w(k, mm_ * 128, (mm_ + 1) * 128), MT[:, k, :], start=(k == 0), stop=(k == KC - 1))
                self.copy(KX[:, m, :], ps[:, 0:NMEM])
        for half in range(2):
            w = self.w_get(l, f'xv_{half}')
            for mt in range(2):
                ps = self.pb()
                for k in range(KC):
                    self.mm(ps[:], MT[:, k, mt * 128:(mt + 1) * 128], w(k, 0, 512), start=(k == 0), stop=(k == KC - 1))
                self.act(XV[:, mt, half * 512:(half + 1) * 512], ps[:], AF.Copy)
        norm_tile = self.rmsnorm_begin(lambda c: P(O_GXA + c))
        def xa_S(hh, G, idx):
            gs = slice(G * T, (G + 1) * T)
            for kt in range(2):
                ps = self.pb()
                for cc in range(2):
                    self.mm(ps[:], KX[:, 2 * hh + cc, kt * 128:(kt + 1) * 128], QX[:, cc, gs], start=(cc == 0), stop=(cc == 1))
                self.act(PX[(idx % 2) * 2 + kt][:], ps[:], AF.Exp, scale=1.0 / 16)

        def xa_F(hh, G, idx):
            gs = slice(G * T, (G + 1) * T)
            pts = [PX[(idx % 2) * 2 + kt] for kt in range(2)]
            rd = RD[idx % 2]
            pd = self.pb()
            for kt in range(2):
                self.mm(pd[:], self.ONESB[:], pts[kt][:], start=(kt == 0), stop=(kt == 1))
            self.act(rd[:], pd[:], AF.Ln)
            self.act(rd[:], rd[:], AF.Exp, scale=-1.0)
            for ec in range(2):
                po = self.pb()
                for kt in range(2):
                    e0 = hh * 256 + ec * 128
                    self.mm(po[:], XV[:, kt, e0:e0 + 128], pts[kt][:], start=(kt == 0), stop=(kt == 1))
                self.tt(OT[:, 2 * hh + ec, gs], po[:], rd[:], ALU.mult)

        pending = None
        idx = 0
        for hh in range(4):
            w = self.w_get(l, f'xq{hh}')
            for cc in range(2):
                for t in range(NT):
                    tt = slice(t * T, (t + 1) * T)
                    if hh == 0 and cc == 0:
                        if t == 0:
                            norm_tile(0)
                        if t + 1 < NT:
                            norm_tile(t + 1)
                    ps = self.pb()
                    for k in range(KC):
                        self.mm(ps[:], w(k, cc * 128, (cc + 1) * 128), self.H[:, k, tt], start=(k == 0), stop=(k == KC - 1))
                    self.copy(QX[:, cc, tt], ps[:])
            for G in range(NT):
                xa_S(hh, G, idx)
                if pending is not None:
                    xa_F(*pending)
                pending = (hh, G, idx)
                idx += 1
        xa_F(*pending)
        self.proj_residual(l, ('xo_0', 'xo_1'), OT)
        self.dump(f'x2_{l}', lambda c: self.X[:, c, :], 8)

        self.sc_off = self.sc_base
        norm_tile = self.rmsnorm_begin(lambda c: P(O_GFFN + c))
        U = self.alloc("U", [128, 4, S_LEN], BF16)
        A2 = [self.alloc(f"A{i}", [128, 2 + T], F32) for i in range(2)]
        CV2 = [self.alloc(f"CV{i}", [128, T], F32) for i in range(2)]
        SL2 = [self.alloc(f"SL{i}", [128, T], F32) for i in range(2)]
        fcnt = 0
        HALO = self.alloc("HALO", [128, 4, 2], F32)
        for g, (c0, n) in enumerate(FGROUPS):
            w1 = self.w_get(l, f'w1_{g}')
            w3 = self.w_get(l, f'w3_{g}', hold_prev=True)
            for t in range(NT):
                tt = slice(t * T, (t + 1) * T)
                if g == 0:
                    if t == 0:
                        norm_tile(0)
                    if t + 1 < NT:
                        norm_tile(t + 1)
                for fi in range(n):
                    fc = c0 + fi
                    pa, p3 = self.pb(), self.pb()
                    for k in range(KC):
                        self.mm(pa[:], w1(k, fi * 128, (fi + 1) * 128), self.H[:, k, tt], start=(k == 0), stop=(k == KC - 1))
                    for k in range(KC):
                        self.mm(p3[:], w3(k, fi * 128, (fi + 1) * 128), self.H[:, k, tt], start=(k == 0), stop=(k == KC - 1))
                    A, CV, SL = A2[fcnt % 2], CV2[fcnt % 2], SL2[fcnt % 2]
                    fcnt += 1
                    if t == 0:
                        self.memset(A[:, 0:2], 0.0)
                    else:
                        self.copy(A[:, 0:2], HALO[:, fi, :])
                    self.act(A[:, 2:2 + T], pa[:], AF.Copy)
                    if t + 1 < NT:
                        self.copy(HALO[:, fi, :], A[:, T:T + 2])
                    self.ts(CV[:], A[:, 2:2 + T], P(O_FCW + 2 * FC + fc), P(O_FCB + fc), ALU.mult, ALU.add)
                    self.stt(CV[:], A[:, 1:1 + T], P(O_FCW + 1 * FC + fc), CV[:], ALU.mult, ALU.add)
                    self.stt(CV[:], A[:, 0:T], P(O_FCW + 0 * FC + fc), CV[:], ALU.mult, ALU.add)
                    self.act(SL[:], CV[:], AF.Silu)
                    self.tt(U[:, fi, tt], SL[:], p3[:], ALU.mult)
            w2 = self.w_get(l, f'w2_{g}')
            for t in range(NT):
                tt = slice(t * T, (t + 1) * T)
                for m in range(KC):
                    ps = self.pb()
                    for fi in range(n):
                        self.mm(ps[:], w2(fi, m * 128, (m + 1) * 128), U[:, fi, tt], start=(fi == 0), stop=(fi == n - 1))
                    self.tt(self.X[:, m, tt], self.X[:, m, tt], ps[:], ALU.add)
        self.dump(f'x3_{l}', lambda c: self.X[:, c, :], 8)

    def final(self, s):
        NT = self.NT
        self.sc_off = self.sc_base
        SQ = [self.alloc(f"fsq{i}", [128, T], BF16) for i in range(2)]
        RS = self.alloc("frs", [128, T], F32)
        OUTB = [self.alloc(f"OUTB{i}", [128, KC, T], F32) for i in range(2)]
        for t in range(NT):
            tt = slice(t * T, (t + 1) * T)
            ps = self.pb()
            for c in range(KC):
                sq = SQ[c % 2]
                self.act(sq[:], self.X[:, c, tt], AF.Square)
                self.mm(ps[:], self.ONESB[:], sq[:], start=(c == 0), stop=(c == KC - 1))
            self.act(RS[:], ps[:], AF.Ln, bias=self.CST[:, 0:1], scale=1.0 / D)
            self.act(RS[:], RS[:], AF.Exp, scale=-0.5)
            ob = OUTB[t % 2]
            for c in range(KC):
                self.stt(ob[:, c, :], self.X[:, c, tt], self.GFIN[:, c:c + 1], RS[:], ALU.mult, ALU.mult)
            self.S.dma('sp', f'out{t % 2}', [(self.outT[s, c * 128:(c + 1) * 128, tt], ob[:, c, :]) for c in range(KC)])


_CACHE = {}


def make_in_maps(inputs, n_cores, nseq, s_len, nl=NLAYER):
    inp = {k: np.asarray(v) for k, v in inputs.items()}
    wts = np.stack([pack_layer_weights(inp, l) for l in range(nl)])
    prm = np.stack([pack_layer_params(inp, l) for l in range(nl)])
    gfin = _cols(inp['final_g'])
    relb = np.ascontiguousarray(inp['rel_bias'], dtype=np.float32)
    oh = onehot_table()
    dalam = np.ascontiguousarray(inp['da_lambda'][:nl].reshape(1, nl * 256))
    maps = []
    for c in range(n_cores):
        xs = inp['x'][c * nseq:(c + 1) * nseq, :s_len]
        ms = inp['mem'][c * nseq:(c + 1) * nseq]
        maps.append({
            "xT": np.ascontiguousarray(xs.transpose(0, 2, 1)),
            "memT": np.ascontiguousarray(ms.transpose(0, 2, 1)),
            "wts": wts, "prm": prm, "gfin": gfin, "relb": relb, "onehot": oh, "dalam": dalam,
            "ident": np.eye(128, dtype=np.float32),
        })
    return maps


def kernel(**inputs):
    n = 8
    nseq = 2
    key = 'full'
    if key not in _CACHE:
        _CACHE[key] = Builder(2048, nseq, NLAYER).build()
    nc = _CACHE[key]
    maps = make_in_maps(inputs, n, nseq, 2048)
    res = run_bass_kernel_spmd(nc, maps, core_ids=list(range(n)))
    outs = [r["outT"].transpose(0, 2, 1) for r in res.results]
    return np.ascontiguousarray(np.concatenate(outs, axis=0).astype(np.float32))
```
